# Optimizing a Trainium2 kernel written in Bass

```python
import jax, jax.numpy as jnp
from jax import lax
import numpy as np

D_MODEL = 2048
BATCH = 4
SEQ = 2048
DEPTH = 1
DEC_BATCH = 2
DEC_SEQ = 4096
PAST_LEN = 128

HEAD_DIM = 128
N_Q_HEADS = 16
N_KV_HEADS = 4
WINDOW = 128
BLOCK = 128
ROPE_DIM = HEAD_DIM // 4
ROPE_THETA = 500000.0
N_REC_HEADS = 16
REC_KEY_DIM = 128
REC_VAL_DIM = 128
CHUNK = 32
D_FF = -(-8 * D_MODEL // (3 * 256)) * 256
RMS_EPS = 1e-6

D_ATT = N_Q_HEADS * HEAD_DIM
D_KV = N_KV_HEADS * HEAD_DIM
D_REC_K = N_REC_HEADS * REC_KEY_DIM
D_REC_V = N_REC_HEADS * REC_VAL_DIM
SPLIT_SIZES = (D_ATT, D_KV, D_KV, D_REC_K, D_REC_K, D_REC_K, D_REC_V, D_REC_V, D_MODEL, D_MODEL)
D_IN = sum(SPLIT_SIZES)
SPLIT_POINTS = [int(v) for v in np.cumsum(SPLIT_SIZES)[:-1]]

kernel_name = "hybrid_window_gqa_hgrn2_encoder"


def rms_norm(x, gain):
    xf = x.astype(jnp.float32)
    y = xf * lax.rsqrt(jnp.mean(xf * xf, axis=-1, keepdims=True) + RMS_EPS)
    return (y * gain.astype(jnp.float32)).astype(x.dtype)


def partial_rope(x):
    L = x.shape[1]
    half = ROPE_DIM // 2
    inv_freq = ROPE_THETA ** (-jnp.arange(half, dtype=jnp.float32) / half)
    ang = jnp.arange(L, dtype=jnp.float32)[:, None] * inv_freq[None, :]
    cos = jnp.cos(ang)[None, :, None, :]
    sin = jnp.sin(ang)[None, :, None, :]
    xf = x.astype(jnp.float32)
    x1 = xf[..., :half]
    x2 = xf[..., half:ROPE_DIM]
    out = jnp.concatenate([x1 * cos - x2 * sin, x2 * cos + x1 * sin, xf[..., ROPE_DIM:]], axis=-1)
    return out.astype(x.dtype)


def window_attention(q, k, v, sink):
    B, L = q.shape[0], q.shape[1]
    nb = L // BLOCK
    G = N_Q_HEADS // N_KV_HEADS
    q = partial_rope(q)
    k = partial_rope(k)
    qb = q.reshape(B, nb, BLOCK, N_KV_HEADS, G, HEAD_DIM)
    pad = ((0, 0), (BLOCK, BLOCK), (0, 0), (0, 0))
    key_idx = jnp.arange(nb)[:, None] * BLOCK + jnp.arange(3 * BLOCK)[None, :]
    kb = jnp.pad(k, pad)[:, key_idx]
    vb = jnp.pad(v, pad)[:, key_idx]
    s = jnp.einsum('bnqhgd,bnkhd->bnhgqk', qb, kb, preferred_element_type=jnp.float32) * (HEAD_DIM ** -0.5)
    q_pos = jnp.arange(nb)[:, None] * BLOCK + jnp.arange(BLOCK)[None, :]
    k_pos = (key_idx - BLOCK)[:, None, :]
    valid = (jnp.abs(q_pos[:, :, None] - k_pos) <= WINDOW) & (k_pos >= 0) & (k_pos < L)
    s = jnp.where(valid[None, :, None, None], s, -jnp.inf)
    sink_b = sink.astype(jnp.float32).reshape(1, 1, N_KV_HEADS, G, 1, 1)
    m = jnp.maximum(jnp.max(s, axis=-1, keepdims=True), sink_b)
    p = jnp.exp(s - m)
    denom = jnp.sum(p, axis=-1, keepdims=True) + jnp.exp(sink_b - m)
    p = (p / denom).astype(v.dtype)
    o = jnp.einsum('bnhgqk,bnkhd->bnqhgd', p, vb)
    return o.reshape(B, L, D_ATT)


def hgrn2_direction(q, k, logf, i):
    B, L, H, DK = q.shape
    DV = i.shape[-1]
    N = L // CHUNK
    qc = q.reshape(B, N, CHUNK, H, DK)
    kc = k.reshape(B, N, CHUNK, H, DK)
    ic = i.reshape(B, N, CHUNK, H, DV)
    b = jnp.cumsum(logf.reshape(B, N, CHUNK, H, DK), axis=2)
    b_ref = b[:, :, CHUNK // 2 - 1:CHUNK // 2]
    b_last = b[:, :, -1]
    q_in = qc * jnp.exp(b - b_ref)
    k_in = kc * jnp.exp(b_ref - b)
    A = jnp.einsum('bnthd,bnshd->bnhts', q_in, k_in)
    lower = jnp.tril(jnp.ones((CHUNK, CHUNK), dtype=bool))
    A = jnp.where(lower, A, 0.0)
    o_intra = jnp.einsum('bnhts,bnshv->bnthv', A, ic)
    k_state = kc * jnp.exp(b_last[:, :, None] - b)
    dS = jnp.einsum('bnshd,bnshv->bnhdv', k_state, ic)
    decay = jnp.exp(b_last)

    def step(S, inp):
        d, ds = inp
        return d[..., None] * S + ds, S

    S0 = jnp.zeros((B, H, DK, DV), dtype=jnp.float32)
    _, S_prev = lax.scan(step, S0, (jnp.moveaxis(decay, 1, 0), jnp.moveaxis(dS, 1, 0)))
    S_prev = jnp.moveaxis(S_prev, 0, 1)
    o_inter = jnp.einsum('bnthd,bnhdv->bnthv', qc * jnp.exp(b), S_prev)
    return (o_intra + o_inter).reshape(B, L, H, DV)


def bidirectional_hgrn2(q_r, zf_fwd, zf_bwd, i_r, lb):
    B, L = q_r.shape[0], q_r.shape[1]
    lbh = lb.reshape(N_REC_HEADS, REC_KEY_DIM)
    q = jax.nn.silu(q_r.astype(jnp.float32)).reshape(B, L, N_REC_HEADS, REC_KEY_DIM)
    i = i_r.astype(jnp.float32).reshape(B, L, N_REC_HEADS, REC_VAL_DIM)

    def gates(z):
        z = z.astype(jnp.float32).reshape(B, L, N_REC_HEADS, REC_KEY_DIM)
        f = lbh + (1.0 - lbh) * jax.nn.sigmoid(z)
        k = (1.0 - lbh) * jax.nn.sigmoid(-z)
        return k, jnp.log(f)

    k_f, logf_f = gates(zf_fwd)
    k_b, logf_b = gates(zf_bwd)
    o_fwd = hgrn2_direction(q, k_f, logf_f, i)
    flip = lambda t: jnp.flip(t, axis=1)
    o_bwd = flip(hgrn2_direction(flip(q), flip(k_b), flip(logf_b), flip(i)))
    return o_fwd + o_bwd


def encoder_layer(x, w_in, sink, rec_norm, lb, w_out, norm_mix_pre, norm_mix_post,
                  norm_ffn_pre, norm_ffn_post, w_gate, w_up, w_down):
    B, L, _ = x.shape
    xn = rms_norm(x, norm_mix_pre)
    proj = xn @ w_in
    q_a, k_a, v_a, q_r, zf_fwd, zf_bwd, i_r, g_r, gate_a, gate_r = jnp.split(proj, SPLIT_POINTS, axis=-1)
    attn = window_attention(q_a.reshape(B, L, N_Q_HEADS, HEAD_DIM),
                            k_a.reshape(B, L, N_KV_HEADS, HEAD_DIM),
                            v_a.reshape(B, L, N_KV_HEADS, HEAD_DIM), sink)
    rec = bidirectional_hgrn2(q_r, zf_fwd, zf_bwd, i_r, lb).astype(x.dtype)
    rec = rms_norm(rec, rec_norm).reshape(B, L, D_REC_V) * jax.nn.silu(g_r)
    merged = jax.nn.sigmoid(gate_a) * attn + jax.nn.sigmoid(gate_r) * rec
    h = x + rms_norm(merged @ w_out, norm_mix_post)
    hn = rms_norm(h, norm_ffn_pre)
    ffn = (jax.nn.silu(hn @ w_gate) * (hn @ w_up)) @ w_down
    return h + rms_norm(ffn, norm_ffn_post)


def setup_inputs(seed: int = 0) -> dict:
    key = jax.random.key(seed)
    ks = jax.random.split(key, 14)
    f32 = jnp.float32
    nrm = lambda k, shape, scale: jax.random.normal(k, shape, f32) * scale
    gain = lambda k, shape: 1.0 + 0.1 * jax.random.normal(k, shape, f32)
    return {
        "x_prompt": jax.random.normal(ks[0], (BATCH, SEQ, D_MODEL), f32),
        "x_sample": jax.random.normal(ks[1], (DEC_BATCH, DEC_SEQ, D_MODEL), f32),
        "w_in": nrm(ks[2], (DEPTH, D_MODEL, D_IN), D_MODEL ** -0.5),
        "sink": nrm(ks[3], (DEPTH, N_Q_HEADS), 0.5),
        "rec_norm": gain(ks[4], (DEPTH, N_REC_HEADS, REC_VAL_DIM)),
        "lb_logits": nrm(ks[5], (DEPTH + 1, D_REC_K), 0.1),
        "w_out": nrm(ks[6], (DEPTH, D_MODEL, D_MODEL), D_MODEL ** -0.5),
        "norm_mix_pre": gain(ks[7], (DEPTH, D_MODEL)),
        "norm_mix_post": gain(ks[8], (DEPTH, D_MODEL)),
        "norm_ffn_pre": gain(ks[9], (DEPTH, D_MODEL)),
        "norm_ffn_post": gain(ks[10], (DEPTH, D_MODEL)),
        "w_gate": nrm(ks[11], (DEPTH, D_MODEL, D_FF), D_MODEL ** -0.5),
        "w_up": nrm(ks[12], (DEPTH, D_MODEL, D_FF), D_MODEL ** -0.5),
        "w_down": nrm(ks[13], (DEPTH, D_FF, D_MODEL), D_FF ** -0.5),
    }


def reference(x_prompt, x_sample, w_in, sink, rec_norm, lb_logits, w_out, norm_mix_pre,
              norm_mix_post, norm_ffn_pre, norm_ffn_post, w_gate, w_up, w_down):
    lb_all = jnp.cumsum(jax.nn.softmax(lb_logits.astype(jnp.float32), axis=0), axis=0)
    y_prompt = x_prompt
    y_sample = x_sample
    for l in range(DEPTH):
        layer_args = (w_in[l], sink[l], rec_norm[l], lb_all[l], w_out[l], norm_mix_pre[l],
                      norm_mix_post[l], norm_ffn_pre[l], norm_ffn_post[l], w_gate[l], w_up[l], w_down[l])
        y_prompt = encoder_layer(y_prompt, *layer_args)
        y_sample = encoder_layer(y_sample, *layer_args)
    return (y_prompt, y_sample)
```

```python
import numpy as np
from contextlib import ExitStack
import concourse.bass as bass
import concourse.mybir as mybir
from concourse.bass_utils import run_bass_kernel_spmd

F32 = mybir.dt.float32
BF16 = mybir.dt.bfloat16
AF = mybir.ActivationFunctionType
ALU = mybir.AluOpType

D = 2048
T = 2048
NB = 2
TE = T + 128 * NB
NT = 16
NTE = NT + NB
TA = T + 128
NTA = NT + 1
DFF = 5632
NFT = DFF // 128
EPS = 1e-6
SCALE = 128 ** -0.5
NW = 4
NCORES = 8
DEBUG = False


class Tk:
    __slots__ = ("sem", "val", "eng")

    def __init__(self, sem, val, eng):
        self.sem = sem
        self.val = val
        self.eng = eng


class Buf:
    __slots__ = ("w", "r")

    def __init__(self):
        self.w = None
        self.r = {}


class DSem:
    __slots__ = ("sem", "val")

    def __init__(self, sem):
        self.sem = sem
        self.val = 0


class Ctx:
    ENG = ("pe", "act", "dve", "pool", "sp")

    def __init__(self, nc, es):
        self.nc = nc
        self.es = es
        self.sem = {e: es.enter_context(nc.semaphore("s_" + e)) for e in self.ENG}
        self.cnt = {e: 0 for e in self.ENG}
        self.seen = {e: {} for e in self.ENG}
        self.dsems = []
        self.prog = {e: [] for e in self.ENG}
        self.nsem = 0

    def dsem(self):
        self.nsem += 1
        d = DSem(self.es.enter_context(self.nc.semaphore("d%d" % self.nsem)))
        self.dsems.append(d)
        return d

    def _wait(self, e, tk):
        if tk is None:
            return
        k = id(tk.sem)
        if self.seen[e].get(k, 0) >= tk.val:
            return
        self.prog[e].append(lambda g, s=tk.sem, v=tk.val: g.wait_ge(s, v))
        self.seen[e][k] = tk.val

    def _deps(self, e, r, w):
        for b in r:
            self._wait(e, b.w)
        for b in w:
            if b.w is not None and b.w.eng != e:
                self._wait(e, b.w)
            for tk in b.r.values():
                if tk.eng != e:
                    self._wait(e, tk)

    def _mark(self, tk, r, w):
        for b in w:
            b.w = tk
            b.r = {}
        for b in r:
            b.r[id(tk.sem)] = tk

    def op(self, e, fn, r=(), w=()):
        self._deps(e, r, w)
        self.cnt[e] += 1
        self.prog[e].append(lambda g, fn=fn, s=self.sem[e]: fn(g).then_inc(s, 1))
        tk = Tk(self.sem[e], self.cnt[e], e)
        self._mark(tk, r, w)
        return tk

    def dma(self, q, ds, out, in_, r=(), w=()):
        self._deps(q, r, w)
        ds.val += 16
        self.prog[q].append(lambda g, o=out, i=in_, s=ds.sem: g.dma_start(out=o, in_=i).then_inc(s, 16))
        tk = Tk(ds.sem, ds.val, "dma")
        self._mark(tk, r, w)
        return tk

    def barrier(self, engines=None):
        for e in (engines or self.ENG):
            for d in self.dsems:
                if d.val:
                    self._wait(e, Tk(d.sem, d.val, "dma"))
            for x in self.ENG:
                if x != e and self.cnt[x]:
                    self._wait(e, Tk(self.sem[x], self.cnt[x], x))

    def emit(self):
        with self.nc.Block() as block:
            def mk(e):
                def f(g):
                    for c in self.prog[e]:
                        c(g)
                return f
            block.tensor(mk("pe"))
            block.scalar(mk("act"))
            block.vector(mk("dve"))
            block.gpsimd(mk("pool"))
            block.sync(mk("sp"))


class Ring:
    def __init__(self, items):
        self.items = items
        self.i = 0

    def next(self):
        it = self.items[self.i % len(self.items)]
        self.i += 1
        return it


def build_program():
    nc = bass.Bass("TRN2", target_bir_lowering=False)

    def din(name, shape, dt=F32):
        return nc.dram_tensor(name, list(shape), dt, kind="ExternalInput").ap()

    x_loc = din("x_loc", [TE, D])
    w_in_t = din("w_in_t", [136, 128, 2048])
    w_out_t = din("w_out_t", [16, 128, 2048])
    wg_t = din("wg_t", [NFT, 128, 2048])
    wu_t = din("wu_t", [NFT, 128, 2048])
    wd_t = din("wd_t", [NFT, 2, 128, 1024])
    vecs = din("vecs", [128, 7 * 16])
    gbc = din("gbc", [128, 2 * D])
    rope_cs = din("rope_cs", [32, 2 * TA])
    cmat = din("cmat", [128, 7 * 128])
    ropeP = din("ropeP", [32, 32])
    y_loc = nc.dram_tensor("y_loc", [T, D], F32, kind="ExternalOutput").ap()
    mg_d = (nc.dram_tensor("mg_d", [16, 128, T], BF16, kind="ExternalOutput") if DEBUG else nc.dram_tensor("mg_d", [16, 128, T], BF16)).ap()
    dbg_ds = []

    def dbg(c, name, ap, shape, dt, bufs):
        if not DEBUG:
            return
        o = nc.dram_tensor("dbg_" + name, list(shape), dt, kind="ExternalOutput").ap()
        if not dbg_ds:
            dbg_ds.append(c.dsem())
        c.dma("sp", dbg_ds[0], o, ap, r=bufs)

    with ExitStack() as es:
        c = Ctx(nc, es)

        def sb(name, shape, dt, stack=None):
            return (stack or es).enter_context(nc.sbuf_tensor(name, list(shape), dt))

        pst = es.enter_context(nc.psum_tensor("ps", [128, 4096], F32))
        psb = [Buf() for _ in range(8)]
        psi = [0]

        def psn():
            k = psi[0] % 8
            psi[0] += 1
            return pst[:, k * 512:(k + 1) * 512], psb[k]

        b_cst = Buf()
        ds_c = c.dsem()
        vec_sb = sb("vec_sb", [128, 7 * 16], F32)
        c.dma("sp", ds_c, vec_sb[:], vecs[:, :], w=[b_cst])
        cm = sb("cm", [128, 7 * 128], BF16)
        ds_c2 = c.dsem()
        c.dma("pool", ds_c2, cm[:], cmat[:, :], w=[b_cst])
        rp = sb("rp", [32, 32], BF16)
        c.dma("pool", ds_c2, rp[:], ropeP[:, :], w=[b_cst])
        ident = cm[:, 0:128]
        ones_bf = cm[:, 128:256]
        mask_f = cm[:, 256:384]
        mask_b = cm[:, 384:512]
        mask_pn = cm[:, 512:768]
        mask_prev = cm[:, 512:640]
        mask_next = cm[:, 640:768]
        mask_halo = cm[:, 768:896]
        gmp = vec_sb[:, 0:16]
        gfp = vec_sb[:, 16:32]
        rn = vec_sb[:, 64:80]
        cst2 = sb("cst2", [128, 7 * 16 + 4], F32)
        lb = cst2[:, 0:16]
        oml = cst2[:, 16:32]
        esk = cst2[:, 32:48]
        tmpc = cst2[:, 48:64]
        epsc = cst2[:, 112:113]
        c.op("dve", lambda g: g.tensor_tensor(out=tmpc, in0=vec_sb[:, 32:48], in1=vec_sb[:, 48:64], op=ALU.subtract), r=[b_cst], w=[b_cst])
        c.op("act", lambda g: g.activation(out=lb, in_=tmpc, func=AF.Sigmoid), r=[b_cst], w=[b_cst])
        c.op("act", lambda g: g.activation(out=oml, in_=tmpc, func=AF.Sigmoid, scale=-1.0), r=[b_cst], w=[b_cst])
        c.op("act", lambda g: g.activation(out=esk, in_=vec_sb[:, 80:96], func=AF.Exp), r=[b_cst], w=[b_cst])
        c.op("pool", lambda g: g.memset(epsc, EPS), w=[b_cst])
        c0c = cst2[:, 64:80]
        c1c = cst2[:, 80:96]
        rnh = cst2[:, 96:112]
        epsc = cst2[:, 112:113]
        onec = cst2[:, 113:114]
        eps4c = cst2[:, 114:115]
        c.op("pool", lambda g: g.memset(epsc, EPS), w=[b_cst])
        c.op("pool", lambda g: g.memset(onec, 1.0), w=[b_cst])
        c.op("pool", lambda g: g.memset(eps4c, 4.0 * EPS), w=[b_cst])
        c.op("dve", lambda g: g.tensor_scalar(out=c1c, in0=oml, scalar1=0.5, scalar2=0.0, op0=ALU.mult, op1=ALU.add), r=[b_cst], w=[b_cst])
        c.op("dve", lambda g: g.tensor_tensor(out=c0c, in0=lb, in1=c1c, op=ALU.add), r=[b_cst], w=[b_cst])
        c.op("dve", lambda g: g.tensor_scalar(out=rnh, in0=rn, scalar1=0.25, scalar2=0.0, op0=ALU.mult, op1=ALU.add), r=[b_cst], w=[b_cst])
        mhalf = sb("mhalf", [128, 512], F32)
        c.op("pool", lambda g: g.memset(mhalf[:], -0.5), w=[b_cst])
        zeros = sb("zeros", [128, 128], F32)
        c.op("pool", lambda g: g.memset(zeros[:], 0.0), w=[b_cst])

        with ExitStack() as es1:
            xnT = sb("xnT", [128, 16, TE], BF16, es1)
            b_xnT = Buf()

            with ExitStack() as esa:
                xb = [(sb("xb%d" % i, [128, D], F32, esa), Buf(), c.dsem()) for i in range(2)]
                xs = [(sb("xs%d" % i, [128, D], BF16, esa), Buf()) for i in range(2)]
                junk = sb("junkA", [128, D], BF16, esa)
                b_junk = Buf()
                ssA = sb("ssA", [128, 2 * NTE], F32, esa)
                b_ssA = Buf()
                for t in range(NTE):
                    xt, bxt, dsx = xb[t % 2]
                    xst, bxs = xs[t % 2]
                    c.dma("sp", dsx, xt[:], x_loc[t * 128:(t + 1) * 128, :], w=[bxt])
                    c.op("act", lambda g, xt=xt, t=t: g.activation(out=junk[:], in_=xt[:], func=AF.Square, accum_out=ssA[:, t:t + 1]),
                         r=[bxt], w=[b_junk, b_ssA])
                    c.op("dve", lambda g, t=t: g.tensor_scalar(out=ssA[:, NTE + t:NTE + t + 1], in0=ssA[:, t:t + 1], scalar1=1.0 / D, scalar2=EPS, op0=ALU.mult, op1=ALU.add),
                         r=[b_ssA], w=[b_ssA])
                    c.op("pool", lambda g, t=t: g.tensor_tensor(out=ssA[:, NTE + t:NTE + t + 1], in0=ssA[:, NTE + t:NTE + t + 1], in1=mhalf[:, 0:1], op=ALU.pow),
                         r=[b_ssA, b_cst], w=[b_ssA])
                    c.op("act", lambda g, xt=xt, xst=xst, t=t: g.activation(out=xst[:], in_=xt[:], func=AF.Copy, scale=ssA[:, NTE + t:NTE + t + 1]),
                         r=[bxt, b_ssA], w=[bxs])
                    for half in range(2):
                        pb, bpb = psn()
                        pbb = pb.bitcast(BF16)
                        for k in range(8):
                            kt = half * 8 + k
                            c.op("pe", lambda g, pbb=pbb, k=k, kt=kt, xst=xst: g.transpose(out=pbb[:, k * 128:(k + 1) * 128], in_=xst[:, kt * 128:(kt + 1) * 128], identity=ident),
                                 r=[bxs, b_cst], w=[bpb])
                        c.op("dve", lambda g, pbb=pbb, half=half, t=t: g.tensor_tensor(
                            out=xnT[:, half * 8:(half + 1) * 8, t * 128:(t + 1) * 128],
                            in0=pbb.rearrange("p (k t) -> p k t", k=8),
                            in1=gmp[:, half * 8:(half + 1) * 8].unsqueeze(2).to_broadcast([128, 8, 128]), op=ALU.mult),
                            r=[bpb, b_cst], w=[b_xnT])
                c.barrier()
            dbg(c, "xnT", xnT[:, :, 0:512], [128, 16, 512], BF16, [b_xnT])

            with ExitStack() as esp:
                def sb1(name, shape, dt):
                    return sb(name, shape, dt, esp)
                wring = Ring([(sb1("w%d" % i, [128, 16, 128], BF16), Buf(), c.dsem()) for i in range(NW)])
                cs_sb = sb1("cs_sb", [32, 2 * TA], F32)
                c.dma("sp", ds_c, cs_sb[:], rope_cs[:, :], w=[b_cst])
                cosT = cs_sb[:, 0:TA]
                sinT = cs_sb[:, TA:2 * TA]
                kT = sb1("kT", [128, TA], BF16); b_kTg = [Buf() for _ in range(5)]
                v_tok = sb1("v_tok", [128, NTA, 128], BF16); b_v = Buf()
                qT = sb1("qT", [128, T], BF16); b_qTg = [Buf() for _ in range(4)]
                AT = sb1("AT", [128, T], BF16); b_AT = Buf()
                ptr = Ring([(sb1("pt%d" % i, [128, 384], BF16), Buf()) for i in range(3)])
                rdr = Ring([(sb1("rd%d" % i, [128, 128], F32), Buf()) for i in range(2)])
                rt1 = Ring([(sb1("rt1_%d" % i, [32, 512], F32), Buf()) for i in range(1)])
                rt2 = Ring([(sb1("rt2_%d" % i, [32, 512], F32), Buf()) for i in range(1)])
                i_tok = sb1("i_tok", [128, NTE, 128], BF16); b_it = Buf()
                qs = sb1("qs", [128, T], BF16); b_qs = Buf()
                Fr = Ring([(sb1("F%d" % i, [128, 512], F32), Buf()) for i in range(2)])
                Er = Ring([(sb1("E%d" % i, [128, 512], F32), Buf()) for i in range(2)])
                QT = sb1("QT", [128, T], BF16); b_QT = Buf()
                KT = sb1("KT", [128, TE], BF16); b_KT = Buf()
                KK = sb1("KK", [128, NTE, 128], BF16); b_KK = Buf()
                dec = sb1("dec", [128, NTE], F32); b_dec = Buf()
                AM = sb1("AM", [128, NT, 128], BF16); b_AM = Buf()
                SB = sb1("SB", [128, NT, 128], BF16); b_SB = Buf()
                Ur = [(sb1("U%d" % i, [128, 128], F32), Buf()) for i in range(2)]
                O = sb1("O", [128, T], F32); b_O = Buf()
                SQr = Ring([(sb1("SQ%d" % i, [128, 512], BF16), Buf()) for i in range(4)])
                RSr = Ring([(sb1("RS%d" % i, [128, 512], F32), Buf()) for i in range(4)])
                G1r = Ring([(sb1("G1_%d" % i, [128, 512], F32), Buf()) for i in range(1)])
                G2r = Ring([(sb1("G2_%d" % i, [128, 512], F32), Buf()) for i in range(1)])
                G3r = Ring([(sb1("G3_%d" % i, [128, 512], F32), Buf()) for i in range(1)])
                TAr = Ring([(sb1("TA_%d" % i, [128, 512], F32), Buf()) for i in range(1)])
                TBr = Ring([(sb1("TB_%d" % i, [128, 512], F32), Buf()) for i in range(1)])
                MG = sb1("MG", [128, T], BF16); b_MG = Buf()
                ds_mg = c.dsem()
                b_mgd = Buf()

                wseq = []
                for ch in range(16):
                    wseq += [24 + ch, 40 + ch, 72 + ch, 56 + ch]
                    if ch % 4 == 0:
                        wseq += [16 + ch // 4, 20 + ch // 4]
                    wseq += [ch, 88 + ch, 120 + ch, 104 + ch]
                wstate = {"issued": 0, "used": 0, "slots": {}}

                def issue_w():
                    i = wstate["issued"]
                    if i >= len(wseq):
                        return
                    wt, bw, dsw = wring.next()
                    c.dma("pool", dsw, wt[:].rearrange("p k j -> p (k j)"), w_in_t[wseq[i]], w=[bw])
                    wstate["slots"][i] = (wt, bw)
                    wstate["issued"] += 1

                def load_w(n, ahead=NW - 1):
                    i = wstate["used"]
                    assert wseq[i] == n, (i, wseq[i], n)
                    while wstate["issued"] < min(len(wseq), i + ahead + 1):
                        issue_w()
                    wstate["used"] += 1
                    return wstate["slots"].pop(i)

                def proj_group(wt, bw, g0, n):
                    pb, bpb = psn()
                    for kt in range(16):
                        c.op("pe", lambda g, pb=pb, kt=kt, g0=g0, n=n, wt=wt: g.matmul(
                            pb[:, 0:n], lhsT=wt[:, kt, :], rhs=xnT[:, kt, g0:g0 + n], start=(kt == 0), stop=(kt == 15)),
                            r=[bw, b_xnT], w=[bpb])
                    return pb, bpb

                def proj_fm(wt, bw, ntok, evac):
                    for g0 in range(0, ntok, 512):
                        n = min(512, ntok - g0)
                        pb, bpb = proj_group(wt, bw, g0, n)
                        evac(pb, bpb, g0, n)

                def proj_tm(wt, bw, ntiles, evac):
                    for j0 in range(0, ntiles, 4):
                        nj = min(4, ntiles - j0)
                        pb, bpb = psn()
                        for jj in range(nj):
                            tt = j0 + jj
                            for kt in range(16):
                                c.op("pe", lambda g, pb=pb, jj=jj, tt=tt, kt=kt, wt=wt: g.matmul(
                                    pb[:, jj * 128:(jj + 1) * 128], lhsT=xnT[:, kt, tt * 128:(tt + 1) * 128], rhs=wt[:, kt, :],
                                    start=(kt == 0), stop=(kt == 15)), r=[bw, b_xnT], w=[bpb])
                        evac(pb, bpb, j0, nj)

                def rope(buf, bbufs, ntok):
                    for g0 in range(0, ntok, 512):
                        n = min(512, ntok - g0)
                        bbuf = bbufs[g0 // 512]
                        pb, bpb = psn()
                        c.op("pe", lambda g, pb=pb, g0=g0, n=n, buf=buf: g.matmul(pb[0:32, 0:n], lhsT=rp[:, :], rhs=buf[0:32, g0:g0 + n], start=True, stop=True),
                             r=[bbuf, b_cst], w=[bpb])
                        t1, bt1 = rt1.next()
                        t2, bt2 = rt2.next()
                        c.op("dve", lambda g, pb=pb, g0=g0, n=n, t1=t1: g.tensor_tensor(out=t1[:, 0:n], in0=pb[0:32, 0:n], in1=sinT[:, g0:g0 + n], op=ALU.mult),
                             r=[bpb, b_cst], w=[bt1])
                        c.op("pool", lambda g, g0=g0, n=n, t2=t2, buf=buf: g.tensor_tensor(out=t2[:, 0:n], in0=buf[0:32, g0:g0 + n], in1=cosT[:, g0:g0 + n], op=ALU.mult),
                             r=[bbuf, b_cst], w=[bt2])
                        c.op("dve", lambda g, g0=g0, n=n, t1=t1, t2=t2, buf=buf: g.tensor_tensor(out=buf[0:32, g0:g0 + n], in0=t1[:, 0:n], in1=t2[:, 0:n], op=ALU.add),
                             r=[bt1, bt2], w=[bbuf])

                b_SBt = [Buf() for _ in range(NT)]

                def proj_fm_gen(wt, bw, ntok, evac, pend=None):
                    for g0 in range(0, ntok, 512):
                        n = min(512, ntok - g0)
                        pb, bpb = proj_group(wt, bw, g0, n)
                        d = evac(pb, bpb, g0, n)
                        if d is not None:
                            pend.append(d)
                        yield

                def proj_tm_gen(wt, bw, ntiles, evac):
                    for j0 in range(0, ntiles, 4):
                        nj = min(4, ntiles - j0)
                        pb, bpb = psn()
                        for jj in range(nj):
                            tt = j0 + jj
                            for kt in range(16):
                                c.op("pe", lambda g, pb=pb, jj=jj, tt=tt, kt=kt, wt=wt: g.matmul(
                                    pb[:, jj * 128:(jj + 1) * 128], lhsT=xnT[:, kt, tt * 128:(tt + 1) * 128], rhs=wt[:, kt, :],
                                    start=(kt == 0), stop=(kt == 15)), r=[bw, b_xnT], w=[bpb])
                        evac(pb, bpb, j0, nj)
                        yield

                def attn_gen(ch):
                    def stage1(n):
                        slots = []
                        if n >= 1:
                            slots.append(n - 1)
                        slots.append(n + 1)
                        slots.append(n)
                        ns = len(slots)
                        pS, bS = psn()
                        for si, kb in enumerate(slots):
                            c.op("pe", lambda g, pS=pS, si=si, kb=kb, n=n: g.matmul(
                                pS[:, si * 128:(si + 1) * 128], lhsT=kT[:, kb * 128:(kb + 1) * 128], rhs=qT[:, n * 128:(n + 1) * 128],
                                start=True, stop=True), r=[b_kTg[kb // 4], b_qTg[n // 4]], w=[bS])
                        PT, bPT = ptr.next()
                        c.op("act", lambda g, PT=PT, pS=pS, ns=ns: g.activation(out=PT[:, 0:ns * 128], in_=pS[:, 0:ns * 128], func=AF.Exp, scale=SCALE),
                             r=[bS], w=[bPT])
                        if n == 0:
                            c.op("pool", lambda g, PT=PT: g.tensor_tensor(out=PT[:, 0:128], in0=PT[:, 0:128], in1=mask_next, op=ALU.mult), r=[bPT, b_cst], w=[bPT])
                        elif n < NT - 1:
                            c.op("pool", lambda g, PT=PT: g.tensor_tensor(out=PT[:, 0:256], in0=PT[:, 0:256], in1=mask_pn, op=ALU.mult), r=[bPT, b_cst], w=[bPT])
                        else:
                            c.op("pool", lambda g, PT=PT: g.tensor_tensor(out=PT[:, 0:128], in0=PT[:, 0:128], in1=mask_prev, op=ALU.mult), r=[bPT, b_cst], w=[bPT])
                            c.op("pool", lambda g, PT=PT: g.tensor_tensor(out=PT[:, 128:256], in0=PT[:, 128:256], in1=mask_halo, op=ALU.mult), r=[bPT, b_cst], w=[bPT])
                        return slots, PT, bPT

                    def stage2(n, slots, PT, bPT):
                        ns = len(slots)
                        pO, bO = psn()
                        for si, kb in enumerate(slots):
                            c.op("pe", lambda g, pO=pO, si=si, kb=kb, PT=PT, ns=ns: g.matmul(
                                pO[:, 0:128], lhsT=v_tok[:, kb, :], rhs=PT[:, si * 128:(si + 1) * 128], start=(si == 0), stop=(si == ns - 1)),
                                r=[b_v, bPT], w=[bO])
                        for si, kb in enumerate(slots):
                            c.op("pe", lambda g, pO=pO, si=si, PT=PT, ns=ns: g.matmul(
                                pO[:, 128:256], lhsT=ones_bf, rhs=PT[:, si * 128:(si + 1) * 128], start=(si == 0), stop=(si == ns - 1)),
                                r=[b_cst, bPT], w=[bO])
                        rd, brd = rdr.next()
                        c.op("dve", lambda g, rd=rd, pO=pO, ch=ch: g.tensor_scalar_add(out=rd[:], in0=pO[:, 128:256], scalar1=esk[:, ch:ch + 1]),
                             r=[bO, b_cst], w=[brd])
                        c.op("dve", lambda g, rd=rd: g.reciprocal(out=rd[:], in_=rd[:]), r=[brd], w=[brd])
                        c.op("dve", lambda g, rd=rd, pO=pO, n=n: g.tensor_tensor(out=AT[:, n * 128:(n + 1) * 128], in0=pO[:, 0:128], in1=rd[:], op=ALU.mult),
                             r=[bO, brd], w=[b_AT])

                    pend = None
                    for n in range(NT + 1):
                        cur = stage1(n) if n < NT else None
                        if pend is not None:
                            stage2(n - 1, *pend)
                        pend = cur
                        yield

                def hgrn_gen(ch):
                    wt, bw = load_w(24 + ch)

                    def qs_evac(pb, bpb, g0, n):
                        Tt, bT = G1r.next()
                        c.op("act", lambda g: g.activation(out=Tt[:, 0:n], in_=pb[:, 0:n], func=AF.Tanh, scale=0.5), r=[bpb], w=[bT])
                        c.op("dve", lambda g: g.scalar_tensor_tensor(out=qs[:, g0:g0 + n], in0=Tt[:, 0:n], scalar=1.0, in1=pb[:, 0:n], op0=ALU.add, op1=ALU.mult),
                             r=[bT, bpb], w=[b_qs])
                        return None
                    yield from proj_fm_gen(wt, bw, T, qs_evac)

                    for p in (1, 2):
                        ntok = T if p == 1 else TE
                        wt, bw = load_w((40 if p == 1 else 56) + ch)
                        rev = (p == 2)

                        def gate_evac(pb, bpb, g0, n, rev=rev):
                            Fg, bF = Fr.next()
                            Eg, bE = Er.next()
                            nj = n // 128
                            j0 = g0 // 128
                            c.op("act", lambda g: g.activation(out=Fg[:, 0:n], in_=pb[:, 0:n], func=AF.Tanh, scale=0.5), r=[bpb], w=[bF])
                            c.op("act", lambda g: g.activation(out=Fg[:, 0:n], in_=Fg[:, 0:n], func=AF.Identity, scale=c1c[:, ch:ch + 1], bias=c0c[:, ch:ch + 1]),
                                 r=[bF, b_cst], w=[bF])
                            for jj in range(nj):
                                if rev:
                                    c.op("dve", lambda g, jj=jj: g.tensor_tensor_scan(
                                        out=Eg[:, jj * 128:(jj + 1) * 128][:, ::-1], data0=Fg[:, jj * 128:(jj + 1) * 128][:, ::-1],
                                        data1=zeros[:, 0:128], initial=1.0, op0=ALU.mult, op1=ALU.add), r=[bF, b_cst], w=[bE])
                                else:
                                    c.op("dve", lambda g, jj=jj: g.tensor_tensor_scan(
                                        out=Eg[:, jj * 128:(jj + 1) * 128], data0=Fg[:, jj * 128:(jj + 1) * 128],
                                        data1=zeros[:, 0:128], initial=1.0, op0=ALU.mult, op1=ALU.add), r=[bF, b_cst], w=[bE])
                            last = 0 if rev else 127
                            c.op("act", lambda g: g.activation(out=dec[:, j0:j0 + nj], in_=Eg[:, 0:n].rearrange("p (j t) -> p j t", t=128)[:, :, last], func=AF.Copy),
                                 r=[bE], w=[b_dec])
                            if g0 < T:
                                c.op("pool", lambda g: g.tensor_tensor(out=QT[:, g0:g0 + n], in0=qs[:, g0:g0 + n], in1=Eg[:, 0:n], op=ALU.mult),
                                     r=[b_qs, bE], w=[b_QT])
                            c.op("act", lambda g: g.activation(out=Fg[:, 0:n], in_=Fg[:, 0:n], func=AF.Identity, scale=-1.0, bias=onec),
                                 r=[bF, b_cst], w=[bF])
                            c.op("dve", lambda g: g.reciprocal(out=Eg[:, 0:n], in_=Eg[:, 0:n]), r=[bE], w=[bE])
                            c.op("pool", lambda g: g.tensor_tensor(out=KT[:, g0:g0 + n], in0=Fg[:, 0:n], in1=Eg[:, 0:n], op=ALU.mult), r=[bF, bE], w=[b_KT])

                            def part_b():
                                pt, bpt = psn()
                                ptb = pt.bitcast(BF16)
                                for jj in range(nj):
                                    c.op("pe", lambda g, jj=jj: g.transpose(out=ptb[:, jj * 128:(jj + 1) * 128], in_=KT[:, g0 + jj * 128:g0 + (jj + 1) * 128], identity=ident),
                                         r=[b_KT, b_cst], w=[bpt])
                                c.op("act", lambda g: g.activation(out=KK[:, j0:j0 + nj, :], in_=ptb[:, 0:nj * 128].rearrange("p (j d) -> p j d", j=nj), func=AF.Copy),
                                     r=[bpt], w=[b_KK])
                            return part_b

                        pend = []
                        yield from proj_fm_gen(wt, bw, ntok, gate_evac, pend)
                        if p == 1:
                            wt, bw = load_w(72 + ch)
                            yield from proj_tm_gen(wt, bw, NTE, lambda pb, bpb, j0, nj: c.op(
                                "act", lambda g: g.activation(out=i_tok[:, j0:j0 + nj, :], in_=pb[:, 0:nj * 128].rearrange("p (j d) -> p j d", j=nj), func=AF.Copy),
                                r=[bpb], w=[b_it]))
                        else:
                            if ch % 4 == 0:
                                h = ch // 4
                                wt, bw = load_w(16 + h)
                                yield from proj_fm_gen(wt, bw, TA, kv_evac)
                                wt, bw = load_w(20 + h)
                                rope(kT, b_kTg, TA)
                                yield from proj_tm_gen(wt, bw, NTA, lambda pb, bpb, j0, nj: c.op(
                                    "dve", lambda g: g.tensor_copy(out=v_tok[:, j0:j0 + nj, :], in_=pb[:, 0:nj * 128].rearrange("p (j d) -> p j d", j=nj)),
                                    r=[bpb], w=[b_v]))
                            wt, bw = load_w(ch)
                            yield from proj_fm_gen(wt, bw, T, q_evac)
                            rope(qT, b_qTg, T)
                            yield "ATTN"
                        for d in pend:
                            d()
                            yield

                        order = list(range(NT)) if p == 1 else list(range(NTE - 1, -1, -1))
                        maskp = mask_f if p == 1 else mask_b
                        for j0 in range(0, NT, 4):
                            pb, bpb = psn()
                            for jj in range(4):
                                j = j0 + jj
                                c.op("pe", lambda g, pb=pb, jj=jj, j=j: g.matmul(
                                    pb[:, jj * 128:(jj + 1) * 128], lhsT=KT[:, j * 128:(j + 1) * 128], rhs=QT[:, j * 128:(j + 1) * 128], start=True, stop=True),
                                    r=[b_KT, b_QT], w=[bpb])
                            c.op("dve", lambda g, pb=pb, j0=j0, maskp=maskp: g.tensor_tensor(
                                out=AM[:, j0:j0 + 4, :], in0=pb.rearrange("p (j t) -> p j t", j=4),
                                in1=maskp.unsqueeze(1).to_broadcast([128, 4, 128]), op=ALU.mult), r=[bpb, b_cst], w=[b_AM])
                            yield
                        pbM = {}
                        for idx in range(0, len(order), 4):
                            pb, bpb = psn()
                            for jj, j in enumerate(order[idx:idx + 4]):
                                c.op("pe", lambda g, pb=pb, jj=jj, j=j: g.matmul(
                                    pb[:, jj * 128:(jj + 1) * 128], lhsT=KK[:, j, :], rhs=i_tok[:, j, :], start=True, stop=True),
                                    r=[b_KK, b_it], w=[bpb])
                                pbM[j] = (pb[:, jj * 128:(jj + 1) * 128], bpb)
                        prev = None
                        for k, j in enumerate(order):
                            Mj, bM = pbM[j]
                            Uc, bUc = Ur[k % 2]
                            if prev is None:
                                c.op("dve", lambda g, Uc=Uc, Mj=Mj: g.tensor_copy(out=Uc[:], in_=Mj), r=[bM], w=[bUc])
                                if j < NT:
                                    c.op("pool", lambda g, j=j: g.memset(SB[:, j, :], 0.0), w=[b_SBt[j]])
                            else:
                                pj, Up, bUp = prev
                                if j < NT:
                                    c.op("act", lambda g, j=j, Up=Up, pj=pj: g.activation(out=SB[:, j, :], in_=Up[:], func=AF.Copy, scale=dec[:, pj:pj + 1]),
                                         r=[bUp, b_dec], w=[b_SBt[j]])
                                c.op("dve", lambda g, Uc=Uc, Up=Up, pj=pj, Mj=Mj: g.scalar_tensor_tensor(
                                    out=Uc[:], in0=Up[:], scalar=dec[:, pj:pj + 1], in1=Mj, op0=ALU.mult, op1=ALU.add),
                                    r=[bUp, b_dec, bM], w=[bUc])
                            prev = (j, Uc, bUc)
                        yield
                        ogroups = list(range(0, NT, 4)) if p == 1 else list(range(NT - 4, -1, -4))
                        for j0 in ogroups:
                            pb, bpb = psn()
                            for jj in range(4):
                                j = j0 + jj
                                c.op("pe", lambda g, pb=pb, jj=jj, j=j: g.matmul(
                                    pb[:, jj * 128:(jj + 1) * 128], lhsT=i_tok[:, j, :], rhs=AM[:, j, :], start=True, stop=False),
                                    r=[b_it, b_AM], w=[bpb])
                                c.op("pe", lambda g, pb=pb, jj=jj, j=j: g.matmul(
                                    pb[:, jj * 128:(jj + 1) * 128], lhsT=SB[:, j, :], rhs=QT[:, j * 128:(j + 1) * 128], start=False, stop=True),
                                    r=[b_SBt[j], b_QT], w=[bpb])
                            if p == 1:
                                c.op("act", lambda g, pb=pb, j0=j0: g.activation(out=O[:, j0 * 128:(j0 + 4) * 128], in_=pb, func=AF.Copy), r=[bpb], w=[b_O])
                            else:
                                c.op("dve", lambda g, pb=pb, j0=j0: g.tensor_tensor(out=O[:, j0 * 128:(j0 + 4) * 128], in0=pb, in1=O[:, j0 * 128:(j0 + 4) * 128], op=ALU.add),
                                     r=[bpb, b_O], w=[b_O])
                            yield

                def drain(gen):
                    for _ in gen:
                        pass

                def kv_evac(pb, bpb, g0, n):
                    c.op("act", lambda g: g.activation(out=kT[:, g0:g0 + n], in_=pb[:, 0:n], func=AF.Copy), r=[bpb], w=[b_kTg[g0 // 512]])
                    return None

                def q_evac(pb, bpb, g0, n):
                    c.op("act", lambda g: g.activation(out=qT[:, g0:g0 + n], in_=pb[:, 0:n], func=AF.Copy), r=[bpb], w=[b_qTg[g0 // 512]])
                    return None

                for ch in range(16):
                    hg = hgrn_gen(ch)
                    at = None
                    for ev in hg:
                        if ev == "ATTN":
                            at = attn_gen(ch)
                        elif at is not None:
                            next(at, None)
                    if ch == 0 and DEBUG:
                        drain(at)
                        dbg(c, "AT", AT[:, :], [128, T], BF16, [b_AT])
                        dbg(c, "O", O[:, :], [128, T], F32, [b_O])
                        dbg(c, "dec2", dec[:, :], [128, NTE], F32, [b_dec])
                        dbg(c, "KK2", KK[:, :, :], [128, NTE, 128], BF16, [b_KK])
                        dbg(c, "KT2", KT[:, :], [128, TE], BF16, [b_KT])
                        dbg(c, "QT2", QT[:, :], [128, T], BF16, [b_QT])
                        dbg(c, "it", i_tok[:, :, :], [128, NTE, 128], BF16, [b_it])
                        dbg(c, "SB2", SB[:, :, :], [128, NT, 128], BF16, b_SBt)

                    wg_, bwg_ = load_w(88 + ch)
                    wr_, bwr_ = load_w(120 + ch, ahead=NW - 2)
                    wa_, bwa_ = load_w(104 + ch, ahead=NW - 3)
                    rs4 = []
                    for gi in range(4):
                        g0 = gi * 512
                        SQ, bSQ = SQr.next()
                        RS, bRS = RSr.next()
                        c.op("act", lambda g, SQ=SQ, g0=g0: g.activation(out=SQ[:], in_=O[:, g0:g0 + 512], func=AF.Square), r=[b_O], w=[bSQ])
                        pb, bpb = psn()
                        c.op("pe", lambda g, pb=pb, SQ=SQ: g.matmul(pb, lhsT=ones_bf, rhs=SQ[:], start=True, stop=True), r=[bSQ, b_cst], w=[bpb])
                        rs4.append((RS, bRS, pb, bpb))
                    for RS, bRS, pb, bpb in rs4:
                        c.op("act", lambda g, pb=pb, RS=RS: g.activation(out=RS[:], in_=pb, func=AF.Sqrt, scale=1.0 / 512, bias=epsc), r=[bpb, b_cst], w=[bRS])
                    for RS, bRS, pb, bpb in rs4:
                        c.op("dve", lambda g, RS=RS: g.reciprocal(out=RS[:], in_=RS[:]), r=[bRS], w=[bRS])
                    for gi in range(4):
                        g0 = gi * 512
                        if gi == 2:
                            drain(at)
                        RS, bRS = rs4[gi][0], rs4[gi][1]
                        G1, bG1 = G1r.next()
                        G2, bG2 = G2r.next()
                        G3, bG3 = G3r.next()
                        TAt, bTA = TAr.next()
                        TBt, bTB = TBr.next()
                        pbg, bpbg = proj_group(wg_, bwg_, g0, 512)
                        if gi < 2:
                            next(at, None)
                        pbr, bpbr = proj_group(wr_, bwr_, g0, 512)
                        if gi < 2:
                            next(at, None)
                        c.op("act", lambda g, pbg=pbg, G1=G1: g.activation(out=G1[:], in_=pbg, func=AF.Tanh, scale=0.5), r=[bpbg], w=[bG1])
                        c.op("dve", lambda g, pbg=pbg, G1=G1: g.scalar_tensor_tensor(out=G1[:], in0=G1[:], scalar=1.0, in1=pbg, op0=ALU.add, op1=ALU.mult), r=[bG1, bpbg], w=[bG1])
                        c.op("dve", lambda g, TAt=TAt, RS=RS, g0=g0: g.tensor_tensor(out=TAt[:], in0=O[:, g0:g0 + 512], in1=RS[:], op=ALU.mult), r=[b_O, bRS], w=[bTA])
                        c.op("dve", lambda g, TAt=TAt, G1=G1, ch=ch: g.scalar_tensor_tensor(out=TAt[:], in0=TAt[:], scalar=rnh[:, ch:ch + 1], in1=G1[:], op0=ALU.mult, op1=ALU.mult),
                             r=[bTA, bG1, b_cst], w=[bTA])
                        pba, bpba = proj_group(wa_, bwa_, g0, 512)
                        c.op("act", lambda g, pbr=pbr, G2=G2: g.activation(out=G2[:], in_=pbr, func=AF.Tanh, scale=0.5), r=[bpbr], w=[bG2])
                        c.op("act", lambda g, G2=G2: g.activation(out=G2[:], in_=G2[:], func=AF.Identity, bias=onec), r=[bG2, b_cst], w=[bG2])
                        c.op("pool", lambda g, TAt=TAt, G2=G2: g.tensor_tensor(out=TAt[:], in0=TAt[:], in1=G2[:], op=ALU.mult), r=[bTA, bG2], w=[bTA])
                        c.op("act", lambda g, pba=pba, G3=G3: g.activation(out=G3[:], in_=pba, func=AF.Tanh, scale=0.5), r=[bpba], w=[bG3])
                        c.op("act", lambda g, G3=G3: g.activation(out=G3[:], in_=G3[:], func=AF.Identity, bias=onec), r=[bG3, b_cst], w=[bG3])
                        c.op("pool", lambda g, TBt=TBt, G3=G3, g0=g0: g.tensor_tensor(out=TBt[:], in0=AT[:, g0:g0 + 512], in1=G3[:], op=ALU.mult), r=[b_AT, bG3], w=[bTB])
                        c.op("pool", lambda g, TAt=TAt, TBt=TBt, g0=g0: g.tensor_tensor(out=MG[:, g0:g0 + 512], in0=TAt[:], in1=TBt[:], op=ALU.add), r=[bTA, bTB], w=[b_MG])
                    c.dma("sp", ds_mg, mg_d[ch], MG[:], r=[b_MG], w=[b_mgd])

                c.barrier()

        with ExitStack() as es2:
            def sb2(name, shape, dt):
                return sb(name, shape, dt, es2)
            R32 = sb2("R32", [128, 8192], F32)
            R32b = R32[:].bitcast(BF16)
            MGs_t = sb2("MGs", [128, 16, 512], BF16)
            MGs = MGs_t[:, :, :]
            hnT = R32b[:, 8192:16384].rearrange("p (k t) -> p k t", k=16)
            FB = R32[:].rearrange("p (t d) -> p t d", t=4)
            b_MGs = Buf(); b_hnT = Buf(); b_FB = Buf()
            HB = sb2("HB", [128, 4, D], F32); b_HB = [Buf() for _ in range(4)]
            actT = sb2("actT", [128, NFT, 512], BF16); b_act = Buf()
            XBr = Ring([(sb2("XB%d" % i, [128, D], F32), Buf(), c.dsem()) for i in range(2)])
            gbc_sb = sb2("gbc_sb", [128, 2 * D], F32)
            c.dma("sp", ds_c, gbc_sb[:], gbc[:, :], w=[b_cst])
            g1bc = gbc_sb[:, 0:D]
            g2bc = gbc_sb[:, D:2 * D]
            HSr = Ring([(sb2("HS%d" % i, [128, D], BF16), Buf()) for i in range(2)])
            junk2 = sb2("junk2", [128, D], BF16); b_j2 = Buf()
            st2 = sb2("st2", [128, 32], F32); b_st2 = Buf()
            SLr = Ring([(sb2("SL%d" % i, [128, 512], F32), Buf()) for i in range(2)])
            w4 = Ring([(sb2("w4_%d" % i, [128, 16, 128], BF16), Buf(), c.dsem()) for i in range(4)])
            w2 = Ring([(sb2("w2_%d" % i, [128, 1024], BF16), Buf(), c.dsem()) for i in range(4)])
            ds_mgl = c.dsem()
            ds_y = [c.dsem() for _ in range(4)]

            seq2 = []
            for G in range(4):
                for pair in range(2):
                    for cc in range(16):
                        seq2.append(("o", cc, pair))
                for ft in range(NFT):
                    seq2.append(("g", ft, 0))
                    seq2.append(("u", ft, 0))
                for half in range(2):
                    for ft in range(NFT):
                        seq2.append(("d", ft, half))
            st = {"issued": 0, "used": 0, "slots": {}}
            DEPTH2 = 8

            def issue2(need=False):
                i = st["issued"]
                if i >= len(seq2):
                    return False
                kind, a, b = seq2[i]
                if kind in ("g", "u", "o"):
                    if w4.i - st.get("u4", 0) >= (4 if need else 3):
                        return False
                    wt, bw, dsw = w4.next()
                    src = (wg_t if kind == "g" else wu_t if kind == "u" else w_out_t)[a]
                    c.dma("pool", dsw, wt[:].rearrange("p k j -> p (k j)"), src, w=[bw])
                else:
                    if w2.i - st.get("u2", 0) >= (4 if need else 3):
                        return False
                    wt, bw, dsw = w2.next()
                    src = wd_t[a, b]
                    c.dma("pool", dsw, wt[:], src, w=[bw])
                st["slots"][i] = (wt, bw)
                st["issued"] += 1
                return True

            def load2(kind, a, b):
                i = st["used"]
                assert seq2[i] == (kind, a, b), (seq2[i], kind, a, b)
                while st["issued"] <= i:
                    assert issue2(True)
                st["used"] += 1
                if kind in ("g", "u", "o"):
                    st["u4"] = st.get("u4", 0) + 1
                else:
                    st["u2"] = st.get("u2", 0) + 1
                while st["issued"] < min(len(seq2), i + DEPTH2):
                    if not issue2():
                        break
                return st["slots"].pop(i)

            def rstd_from(ss_col, out_col, bst, eps=EPS):
                c.op("dve", lambda g: g.tensor_scalar(out=st2[:, out_col:out_col + 1], in0=st2[:, ss_col:ss_col + 1], scalar1=1.0 / D, scalar2=eps, op0=ALU.mult, op1=ALU.add),
                     r=[bst], w=[bst])
                c.op("pool", lambda g: g.tensor_tensor(out=st2[:, out_col:out_col + 1], in0=st2[:, out_col:out_col + 1], in1=mhalf[:, 0:1], op=ALU.pow),
                     r=[bst, b_cst], w=[bst])

            for G in range(4):
                t0 = G * 512
                if G == 0:
                    c.dma("sp", ds_mgl, MGs, mg_d.rearrange("c p t -> p c t")[:, :, t0:t0 + 512], r=[b_mgd], w=[b_MGs])
                def h_chain(tt):
                    XB, b_XB, ds_x = XBr.next()
                    HS, b_HS = HSr.next()
                    bst = Buf()
                    k0 = tt * 8
                    c.dma("sp", ds_x, XB[:], x_loc[t0 + tt * 128:t0 + (tt + 1) * 128, :], w=[b_XB])
                    c.op("act", lambda g, tt=tt, k0=k0: g.activation(out=junk2[:], in_=HB[:, tt, :], func=AF.Square, accum_out=st2[:, k0:k0 + 1]), r=[b_HB[tt]], w=[b_j2, bst])
                    rstd_from(k0, k0 + 1, bst, 4.0 * EPS)
                    c.op("dve", lambda g, tt=tt, k0=k0: g.scalar_tensor_tensor(out=HB[:, tt, :], in0=HB[:, tt, :], scalar=st2[:, k0 + 1:k0 + 2], in1=g1bc, op0=ALU.mult, op1=ALU.mult),
                         r=[b_HB[tt], bst, b_cst], w=[b_HB[tt]])
                    c.op("pool", lambda g, tt=tt, XB=XB: g.tensor_tensor(out=HB[:, tt, :], in0=HB[:, tt, :], in1=XB[:], op=ALU.add), r=[b_HB[tt], b_XB], w=[b_HB[tt]])
                    c.op("act", lambda g, tt=tt, k0=k0: g.activation(out=junk2[:], in_=HB[:, tt, :], func=AF.Square, accum_out=st2[:, k0 + 2:k0 + 3]), r=[b_HB[tt]], w=[b_j2, bst])
                    rstd_from(k0 + 2, k0 + 3, bst)
                    c.op("act", lambda g, tt=tt, k0=k0, HS=HS: g.activation(out=HS[:], in_=HB[:, tt, :], func=AF.Copy, scale=st2[:, k0 + 3:k0 + 4]), r=[b_HB[tt], bst], w=[b_HS])
                    for half in range(2):
                        pb, bpb = psn()
                        pbb = pb.bitcast(BF16)
                        for k in range(8):
                            kt = half * 8 + k
                            c.op("pe", lambda g, pbb=pbb, k=k, kt=kt, HS=HS: g.transpose(out=pbb[:, k * 128:(k + 1) * 128], in_=HS[:, kt * 128:(kt + 1) * 128], identity=ident),
                                 r=[b_HS, b_cst], w=[bpb])
                        c.op("dve", lambda g, pbb=pbb, half=half, tt=tt: g.tensor_tensor(
                            out=hnT[:, half * 8:(half + 1) * 8, tt * 128:(tt + 1) * 128], in0=pbb.rearrange("p (k t) -> p k t", k=8),
                            in1=gfp[:, half * 8:(half + 1) * 8].unsqueeze(2).to_broadcast([128, 8, 128]), op=ALU.mult),
                            r=[bpb, b_cst], w=[b_hnT, b_FB])

                for pair in range(2):
                    tts = (2 * pair, 2 * pair + 1)
                    banks = {tt: [psn() for cg in range(4)] for tt in tts}
                    for cc in range(16):
                        wt, bw = load2("o", cc, pair)
                        wv = wt[:].rearrange("p k j -> p (k j)")
                        for tt in tts:
                            for cg in range(4):
                                pb, bpb = banks[tt][cg]
                                c.op("pe", lambda g, pb=pb, wv=wv, cc=cc, tt=tt, cg=cg: g.matmul(
                                    pb, lhsT=MGs[:, cc, tt * 128:(tt + 1) * 128], rhs=wv[:, cg * 512:(cg + 1) * 512], start=(cc == 0), stop=(cc == 15)),
                                    r=[bw, b_MGs], w=[bpb])
                    for tt in tts:
                        for cg in range(4):
                            pb, bpb = banks[tt][cg]
                            col = cg * 512
                            if cg % 2 == 0:
                                c.op("act", lambda g, pb=pb, tt=tt, col=col: g.activation(out=HB[:, tt, col:col + 512], in_=pb, func=AF.Copy), r=[bpb], w=[b_HB[tt]])
                            else:
                                c.op("dve", lambda g, pb=pb, tt=tt, col=col: g.tensor_copy(out=HB[:, tt, col:col + 512], in_=pb), r=[bpb], w=[b_HB[tt]])
                    if pair == 1 and G < 3:
                        c.dma("sp", ds_mgl, MGs, mg_d.rearrange("c p t -> p c t")[:, :, t0 + 512:t0 + 1024], r=[b_mgd], w=[b_MGs])
                    for tt in tts:
                        h_chain(tt)
                if G == 0:
                    dbg(c, "h", HB[:, :, :], [128, 4, D], F32, b_HB)
                    dbg(c, "hnT", hnT, [128, 16, 512], BF16, [b_hnT])
                for ft in range(NFT):
                    wgt, bwg = load2("g", ft, 0)
                    pg, bpg = psn()
                    for kt in range(16):
                        c.op("pe", lambda g, pg=pg, wgt=wgt, kt=kt: g.matmul(pg, lhsT=wgt[:, kt, :], rhs=hnT[:, kt, :], start=(kt == 0), stop=(kt == 15)),
                             r=[bwg, b_hnT], w=[bpg])
                    wut, bwu = load2("u", ft, 0)
                    pu, bpu = psn()
                    for kt in range(16):
                        c.op("pe", lambda g, pu=pu, wut=wut, kt=kt: g.matmul(pu, lhsT=wut[:, kt, :], rhs=hnT[:, kt, :], start=(kt == 0), stop=(kt == 15)),
                             r=[bwu, b_hnT], w=[bpu])
                    SL, bSL = SLr.next()
                    c.op("act", lambda g, SL=SL, pg=pg: g.activation(out=SL[:], in_=pg, func=AF.Tanh, scale=0.5), r=[bpg], w=[bSL])
                    c.op("dve", lambda g, SL=SL, pg=pg: g.scalar_tensor_tensor(out=SL[:], in0=SL[:], scalar=1.0, in1=pg, op0=ALU.add, op1=ALU.mult), r=[bSL, bpg], w=[bSL])
                    c.op("dve", lambda g, SL=SL, pu=pu, ft=ft: g.scalar_tensor_tensor(out=actT[:, ft, :], in0=SL[:], scalar=0.5, in1=pu, op0=ALU.mult, op1=ALU.mult), r=[bSL, bpu], w=[b_act])
                for half in range(2):
                    banks = [[psn() for cg in range(2)] for tt in range(4)]
                    for ft in range(NFT):
                        wt, bw = load2("d", ft, half)
                        for tt in range(4):
                            for cg in range(2):
                                pb, bpb = banks[tt][cg]
                                c.op("pe", lambda g, pb=pb, wt=wt, ft=ft, tt=tt, cg=cg: g.matmul(
                                    pb, lhsT=actT[:, ft, tt * 128:(tt + 1) * 128], rhs=wt[:, cg * 512:(cg + 1) * 512], start=(ft == 0), stop=(ft == NFT - 1)),
                                    r=[bw, b_act], w=[bpb])
                    for tt in range(4):
                        for cg in range(2):
                            pb, bpb = banks[tt][cg]
                            col = half * 1024 + cg * 512
                            c.op("act", lambda g, pb=pb, tt=tt, col=col: g.activation(out=FB[:, tt, col:col + 512], in_=pb, func=AF.Copy),
                                 r=[bpb], w=[b_FB, b_hnT])
                if G == 0:
                    dbg(c, "ffn", FB, [128, 4, D], F32, [b_FB])
                for tt in range(4):
                    bst = Buf()
                    k0 = tt * 8 + 4
                    c.op("act", lambda g, tt=tt, k0=k0: g.activation(out=junk2[:], in_=FB[:, tt, :], func=AF.Square, accum_out=st2[:, k0:k0 + 1]), r=[b_FB], w=[b_j2, bst])
                    rstd_from(k0, k0 + 1, bst)
                    c.op("dve", lambda g, tt=tt, k0=k0: g.scalar_tensor_tensor(out=FB[:, tt, :], in0=FB[:, tt, :], scalar=st2[:, k0 + 1:k0 + 2], in1=g2bc, op0=ALU.mult, op1=ALU.mult),
                         r=[b_FB, bst, b_cst], w=[b_FB, b_hnT])
                    c.op("pool", lambda g, tt=tt: g.tensor_tensor(out=FB[:, tt, :], in0=FB[:, tt, :], in1=HB[:, tt, :], op=ALU.add),
                         r=[b_FB, b_HB[tt]], w=[b_FB, b_hnT])
                    c.dma("sp", ds_y[tt], y_loc[t0 + tt * 128:t0 + (tt + 1) * 128, :], FB[:, tt, :], r=[b_FB])
            c.barrier(["sp"])
        c.emit()
    return nc


_CACHE = {}


def _host_consts():
    s = np.arange(128)[:, None]
    t = np.arange(128)[None, :]
    ident = (s == t)
    ones = np.ones((128, 128), bool)
    mask_f = (t >= s)
    mask_b = (t <= s)
    mask_prev = (t <= s)
    mask_next = (s <= t)
    return [m.astype(np.float32) for m in (ident, ones, mask_f, mask_b, mask_prev, mask_next)], mask_next.astype(np.float32)


def kernel(x_prompt, x_sample, w_in, sink, rec_norm, lb_logits, w_out, norm_mix_pre,
           norm_mix_post, norm_ffn_pre, norm_ffn_post, w_gate, w_up, w_down):
    f32 = np.float32
    x_prompt = np.asarray(x_prompt, f32)
    x_sample = np.asarray(x_sample, f32)
    w_in = np.asarray(w_in, f32)[0]
    w_out = np.asarray(w_out, f32)[0]
    w_gate = np.asarray(w_gate, f32)[0]
    w_up = np.asarray(w_up, f32)[0]
    w_down = np.asarray(w_down, f32)[0]

    w_in_t = np.ascontiguousarray(w_in.reshape(16, 128, 136, 128).transpose(2, 1, 0, 3)).reshape(136, 128, 2048)
    w_in_t_sw = w_in_t.copy()
    w_in_t_sw[40:56] = w_in_t[56:72]
    w_in_t_sw[56:72] = w_in_t[40:56]
    w_out_t = np.ascontiguousarray(w_out.reshape(16, 128, 2048))
    wg_t = np.ascontiguousarray(w_gate.reshape(16, 128, NFT, 128).transpose(2, 1, 0, 3)).reshape(NFT, 128, 2048)
    wu_t = np.ascontiguousarray(w_up.reshape(16, 128, NFT, 128).transpose(2, 1, 0, 3)).reshape(NFT, 128, 2048)
    wd_t = np.ascontiguousarray(w_down.reshape(NFT, 128, 2, 1024).transpose(0, 2, 1, 3))

    def pk(v):
        return np.asarray(v, f32).reshape(16, 128).T
    vecs = np.zeros((128, 7 * 16), f32)
    vecs[:, 0:16] = pk(norm_mix_pre[0])
    vecs[:, 16:32] = pk(norm_ffn_pre[0])
    vecs[:, 32:48] = pk(lb_logits[0])
    vecs[:, 48:64] = pk(lb_logits[1])
    vecs[:, 64:80] = np.asarray(rec_norm, f32)[0].T
    vecs[:, 80:96] = np.broadcast_to(np.asarray(sink, f32)[0][None, :], (128, 16))
    gbc = np.concatenate([np.broadcast_to(np.asarray(norm_mix_post, f32)[0][None, :], (128, D)),
                          np.broadcast_to(np.asarray(norm_ffn_post, f32)[0][None, :], (128, D))], axis=1).astype(f32)
    gbc = np.ascontiguousarray(gbc)
    mats, tri_next = _host_consts()
    ropeP = np.zeros((32, 32), f32)
    for m in range(16):
        ropeP[m + 16, m] = -1.0
        ropeP[m, m + 16] = 1.0
    inv_freq = (500000.0 ** (-np.arange(16, dtype=np.float32) / 16)).astype(np.float32)

    in_maps = []
    zeros_ext = np.zeros((TE - T, D), f32)
    for core in range(NCORES):
        if core < 4:
            x_loc = np.concatenate([x_prompt[core], zeros_ext], axis=0)
            pos = np.arange(TA, dtype=np.float32)
            halo = np.zeros((128, 128), f32)
            wi = w_in_t
        else:
            b = (core - 4) // 2
            second = (core - 4) % 2 == 1
            if not second:
                x_loc = x_sample[b, 0:TE]
                pos = np.arange(TA, dtype=np.float32)
                wi = w_in_t
            else:
                x_loc = x_sample[b, ::-1][0:TE]
                pos = (4095 - np.arange(TA)).astype(np.float32)
                wi = w_in_t_sw
            halo = tri_next
        ang = pos[None, :] * np.tile(inv_freq, 2)[:, None]
        rope_cs = np.concatenate([np.cos(ang), np.sin(ang)], axis=1).astype(f32)
        cmat = np.concatenate(mats + [halo], axis=1).astype(f32)
        in_maps.append({
            "x_loc": np.ascontiguousarray(x_loc, dtype=f32), "w_in_t": wi, "w_out_t": w_out_t, "wg_t": wg_t, "wu_t": wu_t,
            "wd_t": wd_t, "vecs": vecs, "gbc": gbc, "rope_cs": np.ascontiguousarray(rope_cs), "cmat": np.ascontiguousarray(cmat),
            "ropeP": ropeP,
        })

    if DEBUG:
        return in_maps
    if "nc" not in _CACHE:
        _CACHE["nc"] = build_program()
    res = run_bass_kernel_spmd(_CACHE["nc"], in_maps, core_ids=list(range(NCORES)))
    ys = [np.asarray(r["y_loc"], f32) for r in res.results]
    y_prompt = np.stack(ys[0:4], axis=0)
    y_sample = np.stack([np.concatenate([ys[4 + 2 * b], ys[5 + 2 * b][::-1]], axis=0) for b in range(2)], axis=0)
    return (y_prompt, y_sample)
```

```python
import numpy as np
from contextlib import ExitStack
import concourse.bass as bass
import concourse.mybir as mybir
from concourse.bass_utils import run_bass_kernel_spmd

F32 = mybir.dt.float32
BF16 = mybir.dt.bfloat16
AF = mybir.ActivationFunctionType
ALU = mybir.AluOpType

D = 2048
T = 2048
NB = 2
TE = T + 128 * NB
NT = 16
NTE = NT + NB
TA = T + 128
NTA = NT + 1
DFF = 5632
NFT = DFF // 128
EPS = 1e-6
SCALE = 128 ** -0.5
NW = 4
NCORES = 8
DEBUG = False


class Tk:
    __slots__ = ("sem", "val", "eng")

    def __init__(self, sem, val, eng):
        self.sem = sem
        self.val = val
        self.eng = eng


class Buf:
    __slots__ = ("w", "r")

    def __init__(self):
        self.w = None
        self.r = {}


class DSem:
    __slots__ = ("sem", "val")

    def __init__(self, sem):
        self.sem = sem
        self.val = 0


class Ctx:
    ENG = ("pe", "act", "dve", "pool", "sp")

    def __init__(self, nc, es):
        self.nc = nc
        self.es = es
        self.sem = {e: es.enter_context(nc.semaphore("s_" + e)) for e in self.ENG}
        self.cnt = {e: 0 for e in self.ENG}
        self.seen = {e: {} for e in self.ENG}
        self.dsems = []
        self.prog = {e: [] for e in self.ENG}
        self.nsem = 0

    def dsem(self):
        self.nsem += 1
        d = DSem(self.es.enter_context(self.nc.semaphore("d%d" % self.nsem)))
        self.dsems.append(d)
        return d

    def _wait(self, e, tk):
        if tk is None:
            return
        k = id(tk.sem)
        if self.seen[e].get(k, 0) >= tk.val:
            return
        self.prog[e].append(lambda g, s=tk.sem, v=tk.val: g.wait_ge(s, v))
        self.seen[e][k] = tk.val

    def _deps(self, e, r, w):
        for b in r:
            self._wait(e, b.w)
        for b in w:
            if b.w is not None and (b.w.eng != e or e == "pool"):
                self._wait(e, b.w)
            for tk in b.r.values():
                if tk.eng != e or e == "pool":
                    self._wait(e, tk)

    def _mark(self, tk, r, w):
        for b in w:
            b.w = tk
            b.r = {}
        for b in r:
            b.r[id(tk.sem)] = tk

    def op(self, e, fn, r=(), w=()):
        self._deps(e, r, w)
        self.cnt[e] += 1
        self.prog[e].append(lambda g, fn=fn, s=self.sem[e]: fn(g).then_inc(s, 1))
        tk = Tk(self.sem[e], self.cnt[e], e)
        self._mark(tk, r, w)
        return tk

    def dma(self, q, ds, out, in_, r=(), w=()):
        self._deps(q, r, w)
        ds.val += 16
        self.prog[q].append(lambda g, o=out, i=in_, s=ds.sem: g.dma_start(out=o, in_=i).then_inc(s, 16))
        tk = Tk(ds.sem, ds.val, "dma")
        self._mark(tk, r, w)
        return tk

    def barrier(self, engines=None):
        for e in (engines or self.ENG):
            for d in self.dsems:
                if d.val:
                    self._wait(e, Tk(d.sem, d.val, "dma"))
            for x in self.ENG:
                if x != e and self.cnt[x]:
                    self._wait(e, Tk(self.sem[x], self.cnt[x], x))

    def emit(self):
        with self.nc.Block() as block:
            def mk(e):
                def f(g):
                    for c in self.prog[e]:
                        c(g)
                return f
            block.tensor(mk("pe"))
            block.scalar(mk("act"))
            block.vector(mk("dve"))
            block.gpsimd(mk("pool"))
            block.sync(mk("sp"))


class Ring:
    def __init__(self, items):
        self.items = items
        self.i = 0

    def next(self):
        it = self.items[self.i % len(self.items)]
        self.i += 1
        return it


def build_program():
    nc = bass.Bass("TRN2", target_bir_lowering=False)

    def din(name, shape, dt=F32):
        return nc.dram_tensor(name, list(shape), dt, kind="ExternalInput").ap()

    x_loc = din("x_loc", [TE, D])
    w_in_t = din("w_in_t", [136, 128, 2048])
    w_out_t = din("w_out_t", [16, 128, 2048])
    wg_t = din("wg_t", [NFT, 128, 2048])
    wu_t = din("wu_t", [NFT, 128, 2048])
    wd_t = din("wd_t", [NFT, 2, 128, 1024])
    vecs = din("vecs", [128, 7 * 16])
    gbc = din("gbc", [128, 2 * D])
    rope_cs = din("rope_cs", [32, 2 * TA])
    cmat = din("cmat", [128, 7 * 128])
    ropeP = din("ropeP", [32, 32])
    y_loc = nc.dram_tensor("y_loc", [T, D], F32, kind="ExternalOutput").ap()
    mg_d = (nc.dram_tensor("mg_d", [16, 128, T], BF16, kind="ExternalOutput") if DEBUG else nc.dram_tensor("mg_d", [16, 128, T], BF16)).ap()
    dbg_ds = []

    def dbg(c, name, ap, shape, dt, bufs):
        if not DEBUG:
            return
        o = nc.dram_tensor("dbg_" + name, list(shape), dt, kind="ExternalOutput").ap()
        if not dbg_ds:
            dbg_ds.append(c.dsem())
        c.dma("sp", dbg_ds[0], o, ap, r=bufs)

    with ExitStack() as es:
        c = Ctx(nc, es)

        def sb(name, shape, dt, stack=None):
            return (stack or es).enter_context(nc.sbuf_tensor(name, list(shape), dt))

        pst = es.enter_context(nc.psum_tensor("ps", [128, 4096], F32))
        psb = [Buf() for _ in range(8)]
        psi = [0]

        def psn():
            k = psi[0] % 8
            psi[0] += 1
            return pst[:, k * 512:(k + 1) * 512], psb[k]

        b_cst = Buf()
        ds_c = c.dsem()
        vec_sb = sb("vec_sb", [128, 7 * 16], F32)
        c.dma("sp", ds_c, vec_sb[:], vecs[:, :], w=[b_cst])
        cm = sb("cm", [128, 7 * 128], BF16)
        ds_c2 = c.dsem()
        c.dma("pool", ds_c2, cm[:], cmat[:, :], w=[b_cst])
        rp = sb("rp", [32, 32], BF16)
        c.dma("pool", ds_c2, rp[:], ropeP[:, :], w=[b_cst])
        ident = cm[:, 0:128]
        ones_bf = cm[:, 128:256]
        mask_f = cm[:, 256:384]
        mask_b = cm[:, 384:512]
        mask_pn = cm[:, 512:768]
        mask_prev = cm[:, 512:640]
        mask_next = cm[:, 640:768]
        mask_halo = cm[:, 768:896]
        gmp = vec_sb[:, 0:16]
        gfp = vec_sb[:, 16:32]
        rn = vec_sb[:, 64:80]
        cst2 = sb("cst2", [128, 7 * 16 + 4], F32)
        lb = cst2[:, 0:16]
        oml = cst2[:, 16:32]
        esk = cst2[:, 32:48]
        tmpc = cst2[:, 48:64]
        epsc = cst2[:, 112:113]
        c.op("dve", lambda g: g.tensor_tensor(out=tmpc, in0=vec_sb[:, 32:48], in1=vec_sb[:, 48:64], op=ALU.subtract), r=[b_cst], w=[b_cst])
        c.op("act", lambda g: g.activation(out=lb, in_=tmpc, func=AF.Sigmoid), r=[b_cst], w=[b_cst])
        c.op("act", lambda g: g.activation(out=oml, in_=tmpc, func=AF.Sigmoid, scale=-1.0), r=[b_cst], w=[b_cst])
        c.op("act", lambda g: g.activation(out=esk, in_=vec_sb[:, 80:96], func=AF.Exp), r=[b_cst], w=[b_cst])
        c.op("pool", lambda g: g.memset(epsc, EPS), w=[b_cst])
        c0c = cst2[:, 64:80]
        c1c = cst2[:, 80:96]
        rnh = cst2[:, 96:112]
        epsc = cst2[:, 112:113]
        onec = cst2[:, 113:114]
        eps4c = cst2[:, 114:115]
        c.op("pool", lambda g: g.memset(epsc, EPS), w=[b_cst])
        c.op("pool", lambda g: g.memset(onec, 1.0), w=[b_cst])
        c.op("pool", lambda g: g.memset(eps4c, 4.0 * EPS), w=[b_cst])
        c.op("dve", lambda g: g.tensor_scalar(out=c1c, in0=oml, scalar1=0.5, scalar2=0.0, op0=ALU.mult, op1=ALU.add), r=[b_cst], w=[b_cst])
        c.op("dve", lambda g: g.tensor_tensor(out=c0c, in0=lb, in1=c1c, op=ALU.add), r=[b_cst], w=[b_cst])
        c.op("dve", lambda g: g.tensor_scalar(out=rnh, in0=rn, scalar1=0.25, scalar2=0.0, op0=ALU.mult, op1=ALU.add), r=[b_cst], w=[b_cst])
        mhalf = sb("mhalf", [128, 512], F32)
        c.op("pool", lambda g: g.memset(mhalf[:], -0.5), w=[b_cst])
        zeros = sb("zeros", [128, 128], F32)
        c.op("pool", lambda g: g.memset(zeros[:], 0.0), w=[b_cst])

        with ExitStack() as es1:
            xnT = sb("xnT", [128, 16, TE], BF16, es1)
            b_xnT = Buf()

            with ExitStack() as esa:
                xb = [(sb("xb%d" % i, [128, D], F32, esa), Buf(), c.dsem()) for i in range(2)]
                xs = [(sb("xs%d" % i, [128, D], BF16, esa), Buf()) for i in range(2)]
                junk = sb("junkA", [128, D], BF16, esa)
                b_junk = Buf()
                ssA = sb("ssA", [128, 2 * NTE], F32, esa)
                b_ssA = Buf()
                for t in range(NTE):
                    xt, bxt, dsx = xb[t % 2]
                    xst, bxs = xs[t % 2]
                    c.dma("sp", dsx, xt[:], x_loc[t * 128:(t + 1) * 128, :], w=[bxt])
                    c.op("act", lambda g, xt=xt, t=t: g.activation(out=junk[:], in_=xt[:], func=AF.Square, accum_out=ssA[:, t:t + 1]),
                         r=[bxt], w=[b_junk, b_ssA])
                    c.op("dve", lambda g, t=t: g.tensor_scalar(out=ssA[:, NTE + t:NTE + t + 1], in0=ssA[:, t:t + 1], scalar1=1.0 / D, scalar2=EPS, op0=ALU.mult, op1=ALU.add),
                         r=[b_ssA], w=[b_ssA])
                    c.op("pool", lambda g, t=t: g.tensor_tensor(out=ssA[:, NTE + t:NTE + t + 1], in0=ssA[:, NTE + t:NTE + t + 1], in1=mhalf[:, 0:1], op=ALU.pow),
                         r=[b_ssA, b_cst], w=[b_ssA])
                    c.op("act", lambda g, xt=xt, xst=xst, t=t: g.activation(out=xst[:], in_=xt[:], func=AF.Copy, scale=ssA[:, NTE + t:NTE + t + 1]),
                         r=[bxt, b_ssA], w=[bxs])
                    for half in range(2):
                        pb, bpb = psn()
                        pbb = pb.bitcast(BF16)
                        for k in range(8):
                            kt = half * 8 + k
                            c.op("pe", lambda g, pbb=pbb, k=k, kt=kt, xst=xst: g.transpose(out=pbb[:, k * 128:(k + 1) * 128], in_=xst[:, kt * 128:(kt + 1) * 128], identity=ident),
                                 r=[bxs, b_cst], w=[bpb])
                        c.op("dve", lambda g, pbb=pbb, half=half, t=t: g.tensor_tensor(
                            out=xnT[:, half * 8:(half + 1) * 8, t * 128:(t + 1) * 128],
                            in0=pbb.rearrange("p (k t) -> p k t", k=8),
                            in1=gmp[:, half * 8:(half + 1) * 8].unsqueeze(2).to_broadcast([128, 8, 128]), op=ALU.mult),
                            r=[bpb, b_cst], w=[b_xnT])
                c.barrier()
            dbg(c, "xnT", xnT[:, :, 0:512], [128, 16, 512], BF16, [b_xnT])

            with ExitStack() as esp:
                def sb1(name, shape, dt):
                    return sb(name, shape, dt, esp)
                wring = Ring([(sb1("w%d" % i, [128, 16, 128], BF16), Buf(), c.dsem()) for i in range(NW)])
                cs_sb = sb1("cs_sb", [32, 2 * TA], F32)
                c.dma("sp", ds_c, cs_sb[:], rope_cs[:, :], w=[b_cst])
                cosT = cs_sb[:, 0:TA]
                sinT = cs_sb[:, TA:2 * TA]
                kT = sb1("kT", [128, TA], BF16); b_kTg = [Buf() for _ in range(5)]
                v_tok = sb1("v_tok", [128, NTA, 128], BF16); b_v = Buf()
                qT = sb1("qT", [128, T], BF16); b_qTg = [Buf() for _ in range(4)]
                AT = sb1("AT", [128, T], BF16); b_AT = Buf()
                ptr = Ring([(sb1("pt%d" % i, [128, 384], BF16), Buf()) for i in range(3)])
                rdr = Ring([(sb1("rd%d" % i, [128, 128], F32), Buf()) for i in range(2)])
                rt1 = Ring([(sb1("rt1_%d" % i, [32, 512], F32), Buf()) for i in range(1)])
                rt2 = Ring([(sb1("rt2_%d" % i, [32, 512], F32), Buf()) for i in range(1)])
                i_tok = sb1("i_tok", [128, NTE, 128], BF16); b_it = Buf()
                qs = sb1("qs", [128, T], BF16); b_qs = Buf()
                Fr = Ring([(sb1("F%d" % i, [128, 512], F32), Buf()) for i in range(2)])
                Er = Ring([(sb1("E%d" % i, [128, 512], F32), Buf()) for i in range(2)])
                QT = sb1("QT", [128, T], BF16); b_QT = Buf()
                KT = sb1("KT", [128, TE], BF16); b_KT = Buf()
                KK = sb1("KK", [128, NTE, 128], BF16); b_KK = Buf()
                dec = sb1("dec", [128, NTE], F32); b_dec = Buf()
                AM = sb1("AM", [128, NT, 128], BF16); b_AM = Buf()
                SB = sb1("SB", [128, NT, 128], BF16); b_SB = Buf()
                Ur = [(sb1("U%d" % i, [128, 128], F32), Buf()) for i in range(2)]
                O = sb1("O", [128, T], F32); b_O = Buf()
                SQr = Ring([(sb1("SQ%d" % i, [128, 512], BF16), Buf()) for i in range(4)])
                RSr = Ring([(sb1("RS%d" % i, [128, 512], F32), Buf()) for i in range(4)])
                G1r = Ring([(sb1("G1_%d" % i, [128, 512], F32), Buf()) for i in range(1)])
                G2r = Ring([(sb1("G2_%d" % i, [128, 512], F32), Buf()) for i in range(1)])
                G3r = Ring([(sb1("G3_%d" % i, [128, 512], F32), Buf()) for i in range(1)])
                TAr = Ring([(sb1("TA_%d" % i, [128, 512], F32), Buf()) for i in range(1)])
                TBr = Ring([(sb1("TB_%d" % i, [128, 512], F32), Buf()) for i in range(1)])
                MG = sb1("MG", [128, T], BF16); b_MG = Buf()
                ds_mg = c.dsem()
                b_mgd = Buf()

                wseq = []
                for ch in range(16):
                    wseq += [24 + ch, 40 + ch, 72 + ch, 56 + ch]
                    if ch % 4 == 0:
                        wseq += [16 + ch // 4, 20 + ch // 4]
                    wseq += [ch, 88 + ch, 120 + ch, 104 + ch]
                wstate = {"issued": 0, "used": 0, "slots": {}}

                def issue_w():
                    i = wstate["issued"]
                    if i >= len(wseq):
                        return
                    wt, bw, dsw = wring.next()
                    c.dma("pool", dsw, wt[:].rearrange("p k j -> p (k j)"), w_in_t[wseq[i]], w=[bw])
                    wstate["slots"][i] = (wt, bw)
                    wstate["issued"] += 1

                def load_w(n, ahead=NW - 1):
                    i = wstate["used"]
                    assert wseq[i] == n, (i, wseq[i], n)
                    while wstate["issued"] < min(len(wseq), i + ahead + 1):
                        issue_w()
                    wstate["used"] += 1
                    return wstate["slots"].pop(i)

                def proj_group(wt, bw, g0, n):
                    pb, bpb = psn()
                    for kt in range(16):
                        c.op("pe", lambda g, pb=pb, kt=kt, g0=g0, n=n, wt=wt: g.matmul(
                            pb[:, 0:n], lhsT=wt[:, kt, :], rhs=xnT[:, kt, g0:g0 + n], start=(kt == 0), stop=(kt == 15)),
                            r=[bw, b_xnT], w=[bpb])
                    return pb, bpb

                def proj_fm(wt, bw, ntok, evac):
                    for g0 in range(0, ntok, 512):
                        n = min(512, ntok - g0)
                        pb, bpb = proj_group(wt, bw, g0, n)
                        evac(pb, bpb, g0, n)

                def proj_tm(wt, bw, ntiles, evac):
                    for j0 in range(0, ntiles, 4):
                        nj = min(4, ntiles - j0)
                        pb, bpb = psn()
                        for jj in range(nj):
                            tt = j0 + jj
                            for kt in range(16):
                                c.op("pe", lambda g, pb=pb, jj=jj, tt=tt, kt=kt, wt=wt: g.matmul(
                                    pb[:, jj * 128:(jj + 1) * 128], lhsT=xnT[:, kt, tt * 128:(tt + 1) * 128], rhs=wt[:, kt, :],
                                    start=(kt == 0), stop=(kt == 15)), r=[bw, b_xnT], w=[bpb])
                        evac(pb, bpb, j0, nj)

                def rope(buf, bbufs, ntok):
                    for g0 in range(0, ntok, 512):
                        n = min(512, ntok - g0)
                        bbuf = bbufs[g0 // 512]
                        pb, bpb = psn()
                        c.op("pe", lambda g, pb=pb, g0=g0, n=n, buf=buf: g.matmul(pb[0:32, 0:n], lhsT=rp[:, :], rhs=buf[0:32, g0:g0 + n], start=True, stop=True),
                             r=[bbuf, b_cst], w=[bpb])
                        t1, bt1 = rt1.next()
                        t2, bt2 = rt2.next()
                        c.op("dve", lambda g, pb=pb, g0=g0, n=n, t1=t1: g.tensor_tensor(out=t1[:, 0:n], in0=pb[0:32, 0:n], in1=sinT[:, g0:g0 + n], op=ALU.mult),
                             r=[bpb, b_cst], w=[bt1])
                        c.op("pool", lambda g, g0=g0, n=n, t2=t2, buf=buf: g.tensor_tensor(out=t2[:, 0:n], in0=buf[0:32, g0:g0 + n], in1=cosT[:, g0:g0 + n], op=ALU.mult),
                             r=[bbuf, b_cst], w=[bt2])
                        c.op("dve", lambda g, g0=g0, n=n, t1=t1, t2=t2, buf=buf: g.tensor_tensor(out=buf[0:32, g0:g0 + n], in0=t1[:, 0:n], in1=t2[:, 0:n], op=ALU.add),
                             r=[bt1, bt2], w=[bbuf])

                b_SBt = [Buf() for _ in range(NT)]

                def proj_fm_gen(wt, bw, ntok, evac, pend=None):
                    for g0 in range(0, ntok, 512):
                        n = min(512, ntok - g0)
                        pb, bpb = proj_group(wt, bw, g0, n)
                        d = evac(pb, bpb, g0, n)
                        if d is not None:
                            pend.append(d)
                        yield

                def proj_tm_gen(wt, bw, ntiles, evac):
                    for j0 in range(0, ntiles, 4):
                        nj = min(4, ntiles - j0)
                        pb, bpb = psn()
                        for jj in range(nj):
                            tt = j0 + jj
                            for kt in range(16):
                                c.op("pe", lambda g, pb=pb, jj=jj, tt=tt, kt=kt, wt=wt: g.matmul(
                                    pb[:, jj * 128:(jj + 1) * 128], lhsT=xnT[:, kt, tt * 128:(tt + 1) * 128], rhs=wt[:, kt, :],
                                    start=(kt == 0), stop=(kt == 15)), r=[bw, b_xnT], w=[bpb])
                        evac(pb, bpb, j0, nj)
                        yield

                def attn_gen(ch):
                    def stage1(n):
                        slots = []
                        if n >= 1:
                            slots.append(n - 1)
                        slots.append(n + 1)
                        slots.append(n)
                        ns = len(slots)
                        pS, bS = psn()
                        for si, kb in enumerate(slots):
                            c.op("pe", lambda g, pS=pS, si=si, kb=kb, n=n: g.matmul(
                                pS[:, si * 128:(si + 1) * 128], lhsT=kT[:, kb * 128:(kb + 1) * 128], rhs=qT[:, n * 128:(n + 1) * 128],
                                start=True, stop=True), r=[b_kTg[kb // 4], b_qTg[n // 4]], w=[bS])
                        PT, bPT = ptr.next()
                        c.op("act", lambda g, PT=PT, pS=pS, ns=ns: g.activation(out=PT[:, 0:ns * 128], in_=pS[:, 0:ns * 128], func=AF.Exp, scale=SCALE),
                             r=[bS], w=[bPT])
                        if n == 0:
                            c.op("pool", lambda g, PT=PT: g.tensor_tensor(out=PT[:, 0:128], in0=PT[:, 0:128], in1=mask_next, op=ALU.mult), r=[bPT, b_cst], w=[bPT])
                        elif n < NT - 1:
                            c.op("pool", lambda g, PT=PT: g.tensor_tensor(out=PT[:, 0:256], in0=PT[:, 0:256], in1=mask_pn, op=ALU.mult), r=[bPT, b_cst], w=[bPT])
                        else:
                            c.op("pool", lambda g, PT=PT: g.tensor_tensor(out=PT[:, 0:128], in0=PT[:, 0:128], in1=mask_prev, op=ALU.mult), r=[bPT, b_cst], w=[bPT])
                            c.op("pool", lambda g, PT=PT: g.tensor_tensor(out=PT[:, 128:256], in0=PT[:, 128:256], in1=mask_halo, op=ALU.mult), r=[bPT, b_cst], w=[bPT])
                        return slots, PT, bPT

                    def stage2(n, slots, PT, bPT):
                        ns = len(slots)
                        pO, bO = psn()
                        for si, kb in enumerate(slots):
                            c.op("pe", lambda g, pO=pO, si=si, kb=kb, PT=PT, ns=ns: g.matmul(
                                pO[:, 0:128], lhsT=v_tok[:, kb, :], rhs=PT[:, si * 128:(si + 1) * 128], start=(si == 0), stop=(si == ns - 1)),
                                r=[b_v, bPT], w=[bO])
                        for si, kb in enumerate(slots):
                            c.op("pe", lambda g, pO=pO, si=si, PT=PT, ns=ns: g.matmul(
                                pO[:, 128:256], lhsT=ones_bf, rhs=PT[:, si * 128:(si + 1) * 128], start=(si == 0), stop=(si == ns - 1)),
                                r=[b_cst, bPT], w=[bO])
                        rd, brd = rdr.next()
                        c.op("dve", lambda g, rd=rd, pO=pO, ch=ch: g.tensor_scalar_add(out=rd[:], in0=pO[:, 128:256], scalar1=esk[:, ch:ch + 1]),
                             r=[bO, b_cst], w=[brd])
                        c.op("dve", lambda g, rd=rd: g.reciprocal(out=rd[:], in_=rd[:]), r=[brd], w=[brd])
                        c.op("dve", lambda g, rd=rd, pO=pO, n=n: g.tensor_tensor(out=AT[:, n * 128:(n + 1) * 128], in0=pO[:, 0:128], in1=rd[:], op=ALU.mult),
                             r=[bO, brd], w=[b_AT])

                    pend = None
                    for n in range(NT + 1):
                        cur = stage1(n) if n < NT else None
                        if pend is not None:
                            stage2(n - 1, *pend)
                        pend = cur
                        yield

                def hgrn_gen(ch):
                    wt, bw = load_w(24 + ch)

                    def qs_evac(pb, bpb, g0, n):
                        Tt, bT = G1r.next()
                        c.op("act", lambda g: g.activation(out=Tt[:, 0:n], in_=pb[:, 0:n], func=AF.Tanh, scale=0.5), r=[bpb], w=[bT])
                        c.op("dve", lambda g: g.scalar_tensor_tensor(out=qs[:, g0:g0 + n], in0=Tt[:, 0:n], scalar=1.0, in1=pb[:, 0:n], op0=ALU.add, op1=ALU.mult),
                             r=[bT, bpb], w=[b_qs])
                        return None
                    yield from proj_fm_gen(wt, bw, T, qs_evac)

                    for p in (1, 2):
                        ntok = T if p == 1 else TE
                        wt, bw = load_w((40 if p == 1 else 56) + ch)
                        rev = (p == 2)

                        def gate_evac(pb, bpb, g0, n, rev=rev):
                            Fg, bF = Fr.next()
                            Eg, bE = Er.next()
                            nj = n // 128
                            j0 = g0 // 128
                            c.op("act", lambda g: g.activation(out=Fg[:, 0:n], in_=pb[:, 0:n], func=AF.Tanh, scale=0.5), r=[bpb], w=[bF])
                            c.op("act", lambda g: g.activation(out=Fg[:, 0:n], in_=Fg[:, 0:n], func=AF.Identity, scale=c1c[:, ch:ch + 1], bias=c0c[:, ch:ch + 1]),
                                 r=[bF, b_cst], w=[bF])
                            for jj in range(nj):
                                if rev:
                                    c.op("dve", lambda g, jj=jj: g.tensor_tensor_scan(
                                        out=Eg[:, jj * 128:(jj + 1) * 128][:, ::-1], data0=Fg[:, jj * 128:(jj + 1) * 128][:, ::-1],
                                        data1=zeros[:, 0:128], initial=1.0, op0=ALU.mult, op1=ALU.add), r=[bF, b_cst], w=[bE])
                                else:
                                    c.op("dve", lambda g, jj=jj: g.tensor_tensor_scan(
                                        out=Eg[:, jj * 128:(jj + 1) * 128], data0=Fg[:, jj * 128:(jj + 1) * 128],
                                        data1=zeros[:, 0:128], initial=1.0, op0=ALU.mult, op1=ALU.add), r=[bF, b_cst], w=[bE])
                            last = 0 if rev else 127
                            c.op("act", lambda g: g.activation(out=dec[:, j0:j0 + nj], in_=Eg[:, 0:n].rearrange("p (j t) -> p j t", t=128)[:, :, last], func=AF.Copy),
                                 r=[bE], w=[b_dec])
                            if g0 < T:
                                c.op("pool", lambda g: g.tensor_tensor(out=QT[:, g0:g0 + n], in0=qs[:, g0:g0 + n], in1=Eg[:, 0:n], op=ALU.mult),
                                     r=[b_qs, bE], w=[b_QT])
                            c.op("act", lambda g: g.activation(out=Fg[:, 0:n], in_=Fg[:, 0:n], func=AF.Identity, scale=-1.0, bias=onec),
                                 r=[bF, b_cst], w=[bF])
                            c.op("dve", lambda g: g.reciprocal(out=Eg[:, 0:n], in_=Eg[:, 0:n]), r=[bE], w=[bE])
                            c.op("pool", lambda g: g.tensor_tensor(out=KT[:, g0:g0 + n], in0=Fg[:, 0:n], in1=Eg[:, 0:n], op=ALU.mult), r=[bF, bE], w=[b_KT])

                            def part_b():
                                pt, bpt = psn()
                                ptb = pt.bitcast(BF16)
                                for jj in range(nj):
                                    c.op("pe", lambda g, jj=jj: g.transpose(out=ptb[:, jj * 128:(jj + 1) * 128], in_=KT[:, g0 + jj * 128:g0 + (jj + 1) * 128], identity=ident),
                                         r=[b_KT, b_cst], w=[bpt])
                                c.op("act", lambda g: g.activation(out=KK[:, j0:j0 + nj, :], in_=ptb[:, 0:nj * 128].rearrange("p (j d) -> p j d", j=nj), func=AF.Copy),
                                     r=[bpt], w=[b_KK])
                            return part_b

                        pend = []
                        yield from proj_fm_gen(wt, bw, ntok, gate_evac, pend)
                        if p == 1:
                            wt, bw = load_w(72 + ch)
                            yield from proj_tm_gen(wt, bw, NTE, lambda pb, bpb, j0, nj: c.op(
                                "act", lambda g: g.activation(out=i_tok[:, j0:j0 + nj, :], in_=pb[:, 0:nj * 128].rearrange("p (j d) -> p j d", j=nj), func=AF.Copy),
                                r=[bpb], w=[b_it]))
                        else:
                            if ch % 4 == 0:
                                h = ch // 4
                                wt, bw = load_w(16 + h)
                                yield from proj_fm_gen(wt, bw, TA, kv_evac)
                                wt, bw = load_w(20 + h)
                                rope(kT, b_kTg, TA)
                                yield from proj_tm_gen(wt, bw, NTA, lambda pb, bpb, j0, nj: c.op(
                                    "dve", lambda g: g.tensor_copy(out=v_tok[:, j0:j0 + nj, :], in_=pb[:, 0:nj * 128].rearrange("p (j d) -> p j d", j=nj)),
                                    r=[bpb], w=[b_v]))
                            wt, bw = load_w(ch)
                            yield from proj_fm_gen(wt, bw, T, q_evac)
                            rope(qT, b_qTg, T)
                            yield "ATTN"
                        for d in pend:
                            d()
                            yield

                        order = list(range(NT)) if p == 1 else list(range(NTE - 1, -1, -1))
                        maskp = mask_f if p == 1 else mask_b
                        for j0 in range(0, NT, 4):
                            pb, bpb = psn()
                            for jj in range(4):
                                j = j0 + jj
                                c.op("pe", lambda g, pb=pb, jj=jj, j=j: g.matmul(
                                    pb[:, jj * 128:(jj + 1) * 128], lhsT=KT[:, j * 128:(j + 1) * 128], rhs=QT[:, j * 128:(j + 1) * 128], start=True, stop=True),
                                    r=[b_KT, b_QT], w=[bpb])
                            c.op("dve", lambda g, pb=pb, j0=j0, maskp=maskp: g.tensor_tensor(
                                out=AM[:, j0:j0 + 4, :], in0=pb.rearrange("p (j t) -> p j t", j=4),
                                in1=maskp.unsqueeze(1).to_broadcast([128, 4, 128]), op=ALU.mult), r=[bpb, b_cst], w=[b_AM])
                            yield
                        pbM = {}
                        for idx in range(0, len(order), 4):
                            pb, bpb = psn()
                            for jj, j in enumerate(order[idx:idx + 4]):
                                c.op("pe", lambda g, pb=pb, jj=jj, j=j: g.matmul(
                                    pb[:, jj * 128:(jj + 1) * 128], lhsT=KK[:, j, :], rhs=i_tok[:, j, :], start=True, stop=True),
                                    r=[b_KK, b_it], w=[bpb])
                                pbM[j] = (pb[:, jj * 128:(jj + 1) * 128], bpb)
                        prev = None
                        for k, j in enumerate(order):
                            Mj, bM = pbM[j]
                            Uc, bUc = Ur[k % 2]
                            if prev is None:
                                c.op("dve", lambda g, Uc=Uc, Mj=Mj: g.tensor_copy(out=Uc[:], in_=Mj), r=[bM], w=[bUc])
                                if j < NT:
                                    c.op("pool", lambda g, j=j: g.memset(SB[:, j, :], 0.0), w=[b_SBt[j]])
                            else:
                                pj, Up, bUp = prev
                                if j < NT:
                                    c.op("act", lambda g, j=j, Up=Up, pj=pj: g.activation(out=SB[:, j, :], in_=Up[:], func=AF.Copy, scale=dec[:, pj:pj + 1]),
                                         r=[bUp, b_dec], w=[b_SBt[j]])
                                c.op("dve", lambda g, Uc=Uc, Up=Up, pj=pj, Mj=Mj: g.scalar_tensor_tensor(
                                    out=Uc[:], in0=Up[:], scalar=dec[:, pj:pj + 1], in1=Mj, op0=ALU.mult, op1=ALU.add),
                                    r=[bUp, b_dec, bM], w=[bUc])
                            prev = (j, Uc, bUc)
                        yield
                        ogroups = list(range(0, NT, 4)) if p == 1 else list(range(NT - 4, -1, -4))
                        for j0 in ogroups:
                            pb, bpb = psn()
                            for jj in range(4):
                                j = j0 + jj
                                c.op("pe", lambda g, pb=pb, jj=jj, j=j: g.matmul(
                                    pb[:, jj * 128:(jj + 1) * 128], lhsT=i_tok[:, j, :], rhs=AM[:, j, :], start=True, stop=False),
                                    r=[b_it, b_AM], w=[bpb])
                                c.op("pe", lambda g, pb=pb, jj=jj, j=j: g.matmul(
                                    pb[:, jj * 128:(jj + 1) * 128], lhsT=SB[:, j, :], rhs=QT[:, j * 128:(j + 1) * 128], start=False, stop=True),
                                    r=[b_SBt[j], b_QT], w=[bpb])
                            if p == 1:
                                c.op("act", lambda g, pb=pb, j0=j0: g.activation(out=O[:, j0 * 128:(j0 + 4) * 128], in_=pb, func=AF.Copy), r=[bpb], w=[b_O])
                            else:
                                c.op("dve", lambda g, pb=pb, j0=j0: g.tensor_tensor(out=O[:, j0 * 128:(j0 + 4) * 128], in0=pb, in1=O[:, j0 * 128:(j0 + 4) * 128], op=ALU.add),
                                     r=[bpb, b_O], w=[b_O])
                            yield

                def drain(gen):
                    for _ in gen:
                        pass

                def kv_evac(pb, bpb, g0, n):
                    c.op("act", lambda g: g.activation(out=kT[:, g0:g0 + n], in_=pb[:, 0:n], func=AF.Copy), r=[bpb], w=[b_kTg[g0 // 512]])
                    return None

                def q_evac(pb, bpb, g0, n):
                    c.op("act", lambda g: g.activation(out=qT[:, g0:g0 + n], in_=pb[:, 0:n], func=AF.Copy), r=[bpb], w=[b_qTg[g0 // 512]])
                    return None

                for ch in range(16):
                    hg = hgrn_gen(ch)
                    at = None
                    for ev in hg:
                        if ev == "ATTN":
                            at = attn_gen(ch)
                        elif at is not None:
                            next(at, None)
                    if ch == 0 and DEBUG:
                        drain(at)
                        dbg(c, "AT", AT[:, :], [128, T], BF16, [b_AT])
                        dbg(c, "O", O[:, :], [128, T], F32, [b_O])
                        dbg(c, "dec2", dec[:, :], [128, NTE], F32, [b_dec])
                        dbg(c, "KK2", KK[:, :, :], [128, NTE, 128], BF16, [b_KK])
                        dbg(c, "KT2", KT[:, :], [128, TE], BF16, [b_KT])
                        dbg(c, "QT2", QT[:, :], [128, T], BF16, [b_QT])
                        dbg(c, "it", i_tok[:, :, :], [128, NTE, 128], BF16, [b_it])
                        dbg(c, "SB2", SB[:, :, :], [128, NT, 128], BF16, b_SBt)

                    wg_, bwg_ = load_w(88 + ch)
                    wr_, bwr_ = load_w(120 + ch, ahead=NW - 2)
                    wa_, bwa_ = load_w(104 + ch, ahead=NW - 3)
                    rs4 = []
                    for gi in range(4):
                        g0 = gi * 512
                        SQ, bSQ = SQr.next()
                        RS, bRS = RSr.next()
                        c.op("act", lambda g, SQ=SQ, g0=g0: g.activation(out=SQ[:], in_=O[:, g0:g0 + 512], func=AF.Square), r=[b_O], w=[bSQ])
                        pb, bpb = psn()
                        c.op("pe", lambda g, pb=pb, SQ=SQ: g.matmul(pb, lhsT=ones_bf, rhs=SQ[:], start=True, stop=True), r=[bSQ, b_cst], w=[bpb])
                        rs4.append((RS, bRS, pb, bpb))
                    for RS, bRS, pb, bpb in rs4:
                        c.op("act", lambda g, pb=pb, RS=RS: g.activation(out=RS[:], in_=pb, func=AF.Sqrt, scale=1.0 / 512, bias=epsc), r=[bpb, b_cst], w=[bRS])
                    for RS, bRS, pb, bpb in rs4:
                        c.op("dve", lambda g, RS=RS: g.reciprocal(out=RS[:], in_=RS[:]), r=[bRS], w=[bRS])
                    for gi in range(4):
                        g0 = gi * 512
                        if gi == 2:
                            drain(at)
                        RS, bRS = rs4[gi][0], rs4[gi][1]
                        G1, bG1 = G1r.next()
                        G2, bG2 = G2r.next()
                        G3, bG3 = G3r.next()
                        TAt, bTA = TAr.next()
                        TBt, bTB = TBr.next()
                        pbg, bpbg = proj_group(wg_, bwg_, g0, 512)
                        if gi < 2:
                            next(at, None)
                        pbr, bpbr = proj_group(wr_, bwr_, g0, 512)
                        if gi < 2:
                            next(at, None)
                        c.op("act", lambda g, pbg=pbg, G1=G1: g.activation(out=G1[:], in_=pbg, func=AF.Tanh, scale=0.5), r=[bpbg], w=[bG1])
                        c.op("dve", lambda g, pbg=pbg, G1=G1: g.scalar_tensor_tensor(out=G1[:], in0=G1[:], scalar=1.0, in1=pbg, op0=ALU.add, op1=ALU.mult), r=[bG1, bpbg], w=[bG1])
                        c.op("dve", lambda g, TAt=TAt, RS=RS, g0=g0: g.tensor_tensor(out=TAt[:], in0=O[:, g0:g0 + 512], in1=RS[:], op=ALU.mult), r=[b_O, bRS], w=[bTA])
                        c.op("dve", lambda g, TAt=TAt, G1=G1, ch=ch: g.scalar_tensor_tensor(out=TAt[:], in0=TAt[:], scalar=rnh[:, ch:ch + 1], in1=G1[:], op0=ALU.mult, op1=ALU.mult),
                             r=[bTA, bG1, b_cst], w=[bTA])
                        pba, bpba = proj_group(wa_, bwa_, g0, 512)
                        c.op("act", lambda g, pbr=pbr, G2=G2: g.activation(out=G2[:], in_=pbr, func=AF.Tanh, scale=0.5), r=[bpbr], w=[bG2])
                        c.op("act", lambda g, G2=G2: g.activation(out=G2[:], in_=G2[:], func=AF.Identity, bias=onec), r=[bG2, b_cst], w=[bG2])
                        c.op("pool", lambda g, TAt=TAt, G2=G2: g.tensor_tensor(out=TAt[:], in0=TAt[:], in1=G2[:], op=ALU.mult), r=[bTA, bG2], w=[bTA])
                        c.op("act", lambda g, pba=pba, G3=G3: g.activation(out=G3[:], in_=pba, func=AF.Tanh, scale=0.5), r=[bpba], w=[bG3])
                        c.op("act", lambda g, G3=G3: g.activation(out=G3[:], in_=G3[:], func=AF.Identity, bias=onec), r=[bG3, b_cst], w=[bG3])
                        c.op("pool", lambda g, TBt=TBt, G3=G3, g0=g0: g.tensor_tensor(out=TBt[:], in0=AT[:, g0:g0 + 512], in1=G3[:], op=ALU.mult), r=[b_AT, bG3], w=[bTB])
                        c.op("pool", lambda g, TAt=TAt, TBt=TBt, g0=g0: g.tensor_tensor(out=MG[:, g0:g0 + 512], in0=TAt[:], in1=TBt[:], op=ALU.add), r=[bTA, bTB], w=[b_MG])
                    c.dma("sp", ds_mg, mg_d[ch], MG[:], r=[b_MG], w=[b_mgd])

                c.barrier()

        with ExitStack() as es2:
            def sb2(name, shape, dt):
                return sb(name, shape, dt, es2)
            R32 = sb2("R32", [128, 8192], F32)
            R32b = R32[:].bitcast(BF16)
            MGs_t = sb2("MGs", [128, 16, 512], BF16)
            MGs = MGs_t[:, :, :]
            hnT = R32b[:, 8192:16384].rearrange("p (k t) -> p k t", k=16)
            FB = R32[:].rearrange("p (t d) -> p t d", t=4)
            b_MGs = Buf(); b_hnT = Buf(); b_FB = Buf()
            HB = sb2("HB", [128, 4, D], F32); b_HB = [Buf() for _ in range(4)]
            actT = sb2("actT", [128, NFT, 512], BF16); b_act = Buf()
            XBr = Ring([(sb2("XB%d" % i, [128, D], F32), Buf(), c.dsem()) for i in range(2)])
            gbc_sb = sb2("gbc_sb", [128, 2 * D], F32)
            c.dma("sp", ds_c, gbc_sb[:], gbc[:, :], w=[b_cst])
            g1bc = gbc_sb[:, 0:D]
            g2bc = gbc_sb[:, D:2 * D]
            HSr = Ring([(sb2("HS%d" % i, [128, D], BF16), Buf()) for i in range(2)])
            junk2 = sb2("junk2", [128, D], BF16); b_j2 = Buf()
            st2 = sb2("st2", [128, 32], F32); b_st2 = Buf()
            SLr = Ring([(sb2("SL%d" % i, [128, 512], F32), Buf()) for i in range(2)])
            w4 = Ring([(sb2("w4_%d" % i, [128, 16, 128], BF16), Buf(), c.dsem()) for i in range(4)])
            w2 = Ring([(sb2("w2_%d" % i, [128, 1024], BF16), Buf(), c.dsem()) for i in range(4)])
            ds_mgl = c.dsem()
            ds_y = [c.dsem() for _ in range(4)]

            seq2 = []
            for G in range(4):
                for pair in range(2):
                    for cc in range(16):
                        seq2.append(("o", cc, pair))
                for ft in range(NFT):
                    seq2.append(("g", ft, 0))
                    seq2.append(("u", ft, 0))
                for half in range(2):
                    for ft in range(NFT):
                        seq2.append(("d", ft, half))
            st = {"issued": 0, "used": 0, "slots": {}}
            DEPTH2 = 8

            def issue2(need=False):
                i = st["issued"]
                if i >= len(seq2):
                    return False
                kind, a, b = seq2[i]
                if kind in ("g", "u", "o"):
                    if w4.i - st.get("u4", 0) >= (4 if need else 3):
                        return False
                    wt, bw, dsw = w4.next()
                    src = (wg_t if kind == "g" else wu_t if kind == "u" else w_out_t)[a]
                    c.dma("pool", dsw, wt[:].rearrange("p k j -> p (k j)"), src, w=[bw])
                else:
                    if w2.i - st.get("u2", 0) >= (4 if need else 3):
                        return False
                    wt, bw, dsw = w2.next()
                    src = wd_t[a, b]
                    c.dma("pool", dsw, wt[:], src, w=[bw])
                st["slots"][i] = (wt, bw)
                st["issued"] += 1
                return True

            def load2(kind, a, b):
                i = st["used"]
                assert seq2[i] == (kind, a, b), (seq2[i], kind, a, b)
                while st["issued"] <= i:
                    assert issue2(True)
                st["used"] += 1
                if kind in ("g", "u", "o"):
                    st["u4"] = st.get("u4", 0) + 1
                else:
                    st["u2"] = st.get("u2", 0) + 1
                while st["issued"] < min(len(seq2), i + DEPTH2):
                    if not issue2():
                        break
                return st["slots"].pop(i)

            def rstd_from(ss_col, out_col, bst, eps=EPS):
                epa = eps4c if eps != EPS else epsc
                c.op("act", lambda g: g.activation(out=st2[:, out_col:out_col + 1], in_=st2[:, ss_col:ss_col + 1], func=AF.Sqrt, scale=1.0 / D, bias=epa),
                     r=[bst, b_cst], w=[bst])
                c.op("dve", lambda g: g.reciprocal(out=st2[:, out_col:out_col + 1], in_=st2[:, out_col:out_col + 1]), r=[bst], w=[bst])

            for G in range(4):
                t0 = G * 512
                if G == 0:
                    c.dma("sp", ds_mgl, MGs, mg_d.rearrange("c p t -> p c t")[:, :, t0:t0 + 512], r=[b_mgd], w=[b_MGs])
                def h_chain(tt):
                    XB, b_XB, ds_x = XBr.next()
                    HS, b_HS = HSr.next()
                    bst = Buf()
                    k0 = tt * 8
                    c.dma("sp", ds_x, XB[:], x_loc[t0 + tt * 128:t0 + (tt + 1) * 128, :], w=[b_XB])
                    c.op("act", lambda g, tt=tt, k0=k0: g.activation(out=junk2[:], in_=HB[:, tt, :], func=AF.Square, accum_out=st2[:, k0:k0 + 1]), r=[b_HB[tt]], w=[b_j2, bst])
                    rstd_from(k0, k0 + 1, bst, 4.0 * EPS)
                    c.op("dve", lambda g, tt=tt, k0=k0: g.scalar_tensor_tensor(out=HB[:, tt, :], in0=HB[:, tt, :], scalar=st2[:, k0 + 1:k0 + 2], in1=g1bc, op0=ALU.mult, op1=ALU.mult),
                         r=[b_HB[tt], bst, b_cst], w=[b_HB[tt]])
                    c.op("dve", lambda g, tt=tt, XB=XB: g.tensor_tensor(out=HB[:, tt, :], in0=HB[:, tt, :], in1=XB[:], op=ALU.add), r=[b_HB[tt], b_XB], w=[b_HB[tt]])
                    c.op("act", lambda g, tt=tt, k0=k0: g.activation(out=junk2[:], in_=HB[:, tt, :], func=AF.Square, accum_out=st2[:, k0 + 2:k0 + 3]), r=[b_HB[tt]], w=[b_j2, bst])
                    rstd_from(k0 + 2, k0 + 3, bst)
                    c.op("act", lambda g, tt=tt, k0=k0, HS=HS: g.activation(out=HS[:], in_=HB[:, tt, :], func=AF.Copy, scale=st2[:, k0 + 3:k0 + 4]), r=[b_HB[tt], bst], w=[b_HS])
                    for half in range(2):
                        pb, bpb = psn()
                        pbb = pb.bitcast(BF16)
                        for k in range(8):
                            kt = half * 8 + k
                            c.op("pe", lambda g, pbb=pbb, k=k, kt=kt, HS=HS: g.transpose(out=pbb[:, k * 128:(k + 1) * 128], in_=HS[:, kt * 128:(kt + 1) * 128], identity=ident),
                                 r=[b_HS, b_cst], w=[bpb])
                        c.op("dve", lambda g, pbb=pbb, half=half, tt=tt: g.tensor_tensor(
                            out=hnT[:, half * 8:(half + 1) * 8, tt * 128:(tt + 1) * 128], in0=pbb.rearrange("p (k t) -> p k t", k=8),
                            in1=gfp[:, half * 8:(half + 1) * 8].unsqueeze(2).to_broadcast([128, 8, 128]), op=ALU.mult),
                            r=[bpb, b_cst], w=[b_hnT, b_FB])

                for pair in range(2):
                    tts = (2 * pair, 2 * pair + 1)
                    banks = {tt: [psn() for cg in range(4)] for tt in tts}
                    for cc in range(16):
                        wt, bw = load2("o", cc, pair)
                        wv = wt[:].rearrange("p k j -> p (k j)")
                        for tt in tts:
                            for cg in range(4):
                                pb, bpb = banks[tt][cg]
                                c.op("pe", lambda g, pb=pb, wv=wv, cc=cc, tt=tt, cg=cg: g.matmul(
                                    pb, lhsT=MGs[:, cc, tt * 128:(tt + 1) * 128], rhs=wv[:, cg * 512:(cg + 1) * 512], start=(cc == 0), stop=(cc == 15)),
                                    r=[bw, b_MGs], w=[bpb])
                    for tt in tts:
                        for cg in range(4):
                            pb, bpb = banks[tt][cg]
                            col = cg * 512
                            if cg % 2 == 0:
                                c.op("act", lambda g, pb=pb, tt=tt, col=col: g.activation(out=HB[:, tt, col:col + 512], in_=pb, func=AF.Copy), r=[bpb], w=[b_HB[tt]])
                            else:
                                c.op("dve", lambda g, pb=pb, tt=tt, col=col: g.tensor_copy(out=HB[:, tt, col:col + 512], in_=pb), r=[bpb], w=[b_HB[tt]])
                    if pair == 1 and G < 3:
                        c.dma("sp", ds_mgl, MGs, mg_d.rearrange("c p t -> p c t")[:, :, t0 + 512:t0 + 1024], r=[b_mgd], w=[b_MGs])
                    for tt in tts:
                        h_chain(tt)
                if G == 0:
                    dbg(c, "h", HB[:, :, :], [128, 4, D], F32, b_HB)
                    dbg(c, "hnT", hnT, [128, 16, 512], BF16, [b_hnT])
                for ft in range(NFT):
                    wgt, bwg = load2("g", ft, 0)
                    pg, bpg = psn()
                    for kt in range(16):
                        c.op("pe", lambda g, pg=pg, wgt=wgt, kt=kt: g.matmul(pg, lhsT=wgt[:, kt, :], rhs=hnT[:, kt, :], start=(kt == 0), stop=(kt == 15)),
                             r=[bwg, b_hnT], w=[bpg])
                    wut, bwu = load2("u", ft, 0)
                    pu, bpu = psn()
                    for kt in range(16):
                        c.op("pe", lambda g, pu=pu, wut=wut, kt=kt: g.matmul(pu, lhsT=wut[:, kt, :], rhs=hnT[:, kt, :], start=(kt == 0), stop=(kt == 15)),
                             r=[bwu, b_hnT], w=[bpu])
                    SL, bSL = SLr.next()
                    c.op("act", lambda g, SL=SL, pg=pg: g.activation(out=SL[:], in_=pg, func=AF.Tanh, scale=0.5), r=[bpg], w=[bSL])
                    c.op("dve", lambda g, SL=SL, pg=pg: g.scalar_tensor_tensor(out=SL[:], in0=SL[:], scalar=1.0, in1=pg, op0=ALU.add, op1=ALU.mult), r=[bSL, bpg], w=[bSL])
                    c.op("dve", lambda g, SL=SL, pu=pu, ft=ft: g.scalar_tensor_tensor(out=actT[:, ft, :], in0=SL[:], scalar=0.5, in1=pu, op0=ALU.mult, op1=ALU.mult), r=[bSL, bpu], w=[b_act])
                for half in range(2):
                    banks = [[psn() for cg in range(2)] for tt in range(4)]
                    for ft in range(NFT):
                        wt, bw = load2("d", ft, half)
                        for tt in range(4):
                            for cg in range(2):
                                pb, bpb = banks[tt][cg]
                                c.op("pe", lambda g, pb=pb, wt=wt, ft=ft, tt=tt, cg=cg: g.matmul(
                                    pb, lhsT=actT[:, ft, tt * 128:(tt + 1) * 128], rhs=wt[:, cg * 512:(cg + 1) * 512], start=(ft == 0), stop=(ft == NFT - 1)),
                                    r=[bw, b_act], w=[bpb])
                    for tt in range(4):
                        for cg in range(2):
                            pb, bpb = banks[tt][cg]
                            col = half * 1024 + cg * 512
                            c.op("act", lambda g, pb=pb, tt=tt, col=col: g.activation(out=FB[:, tt, col:col + 512], in_=pb, func=AF.Copy),
                                 r=[bpb], w=[b_FB, b_hnT])
                if G == 0:
                    dbg(c, "ffn", FB, [128, 4, D], F32, [b_FB])
                for tt in range(4):
                    bst = Buf()
                    k0 = tt * 8 + 4
                    c.op("act", lambda g, tt=tt, k0=k0: g.activation(out=junk2[:], in_=FB[:, tt, :], func=AF.Square, accum_out=st2[:, k0:k0 + 1]), r=[b_FB], w=[b_j2, bst])
                    rstd_from(k0, k0 + 1, bst)
                    c.op("dve", lambda g, tt=tt, k0=k0: g.scalar_tensor_tensor(out=FB[:, tt, :], in0=FB[:, tt, :], scalar=st2[:, k0 + 1:k0 + 2], in1=g2bc, op0=ALU.mult, op1=ALU.mult),
                         r=[b_FB, bst, b_cst], w=[b_FB, b_hnT])
                    c.op("dve", lambda g, tt=tt: g.tensor_tensor(out=FB[:, tt, :], in0=FB[:, tt, :], in1=HB[:, tt, :], op=ALU.add),
                         r=[b_FB, b_HB[tt]], w=[b_FB, b_hnT])
                    c.dma("sp", ds_y[tt], y_loc[t0 + tt * 128:t0 + (tt + 1) * 128, :], FB[:, tt, :], r=[b_FB])
            c.barrier(["sp"])
        c.emit()
    return nc


_CACHE = {}


def _host_consts():
    s = np.arange(128)[:, None]
    t = np.arange(128)[None, :]
    ident = (s == t)
    ones = np.ones((128, 128), bool)
    mask_f = (t >= s)
    mask_b = (t <= s)
    mask_prev = (t <= s)
    mask_next = (s <= t)
    return [m.astype(np.float32) for m in (ident, ones, mask_f, mask_b, mask_prev, mask_next)], mask_next.astype(np.float32)


def kernel(x_prompt, x_sample, w_in, sink, rec_norm, lb_logits, w_out, norm_mix_pre,
           norm_mix_post, norm_ffn_pre, norm_ffn_post, w_gate, w_up, w_down):
    f32 = np.float32
    x_prompt = np.asarray(x_prompt, f32)
    x_sample = np.asarray(x_sample, f32)
    w_in = np.asarray(w_in, f32)[0]
    w_out = np.asarray(w_out, f32)[0]
    w_gate = np.asarray(w_gate, f32)[0]
    w_up = np.asarray(w_up, f32)[0]
    w_down = np.asarray(w_down, f32)[0]

    w_in_t = np.ascontiguousarray(w_in.reshape(16, 128, 136, 128).transpose(2, 1, 0, 3)).reshape(136, 128, 2048)
    w_in_t_sw = w_in_t.copy()
    w_in_t_sw[40:56] = w_in_t[56:72]
    w_in_t_sw[56:72] = w_in_t[40:56]
    w_out_t = np.ascontiguousarray(w_out.reshape(16, 128, 2048))
    wg_t = np.ascontiguousarray(w_gate.reshape(16, 128, NFT, 128).transpose(2, 1, 0, 3)).reshape(NFT, 128, 2048)
    wu_t = np.ascontiguousarray(w_up.reshape(16, 128, NFT, 128).transpose(2, 1, 0, 3)).reshape(NFT, 128, 2048)
    wd_t = np.ascontiguousarray(w_down.reshape(NFT, 128, 2, 1024).transpose(0, 2, 1, 3))

    def pk(v):
        return np.asarray(v, f32).reshape(16, 128).T
    vecs = np.zeros((128, 7 * 16), f32)
    vecs[:, 0:16] = pk(norm_mix_pre[0])
    vecs[:, 16:32] = pk(norm_ffn_pre[0])
    vecs[:, 32:48] = pk(lb_logits[0])
    vecs[:, 48:64] = pk(lb_logits[1])
    vecs[:, 64:80] = np.asarray(rec_norm, f32)[0].T
    vecs[:, 80:96] = np.broadcast_to(np.asarray(sink, f32)[0][None, :], (128, 16))
    gbc = np.concatenate([np.broadcast_to(np.asarray(norm_mix_post, f32)[0][None, :], (128, D)),
                          np.broadcast_to(np.asarray(norm_ffn_post, f32)[0][None, :], (128, D))], axis=1).astype(f32)
    gbc = np.ascontiguousarray(gbc)
    mats, tri_next = _host_consts()
    ropeP = np.zeros((32, 32), f32)
    for m in range(16):
        ropeP[m + 16, m] = -1.0
        ropeP[m, m + 16] = 1.0
    inv_freq = (500000.0 ** (-np.arange(16, dtype=np.float32) / 16)).astype(np.float32)

    in_maps = []
    zeros_ext = np.zeros((TE - T, D), f32)
    for core in range(NCORES):
        if core < 4:
            x_loc = np.concatenate([x_prompt[core], zeros_ext], axis=0)
            pos = np.arange(TA, dtype=np.float32)
            halo = np.zeros((128, 128), f32)
            wi = w_in_t
        else:
            b = (core - 4) // 2
            second = (core - 4) % 2 == 1
            if not second:
                x_loc = x_sample[b, 0:TE]
                pos = np.arange(TA, dtype=np.float32)
                wi = w_in_t
            else:
                x_loc = x_sample[b, ::-1][0:TE]
                pos = (4095 - np.arange(TA)).astype(np.float32)
                wi = w_in_t_sw
            halo = tri_next
        ang = pos[None, :] * np.tile(inv_freq, 2)[:, None]
        rope_cs = np.concatenate([np.cos(ang), np.sin(ang)], axis=1).astype(f32)
        cmat = np.concatenate(mats + [halo], axis=1).astype(f32)
        in_maps.append({
            "x_loc": np.ascontiguousarray(x_loc, dtype=f32), "w_in_t": wi, "w_out_t": w_out_t, "wg_t": wg_t, "wu_t": wu_t,
            "wd_t": wd_t, "vecs": vecs, "gbc": gbc, "rope_cs": np.ascontiguousarray(rope_cs), "cmat": np.ascontiguousarray(cmat),
            "ropeP": ropeP,
        })

    if DEBUG:
        return in_maps
    if "nc" not in _CACHE:
        _CACHE["nc"] = build_program()
    res = run_bass_kernel_spmd(_CACHE["nc"], in_maps, core_ids=list(range(NCORES)))
    ys = [np.asarray(r["y_loc"], f32) for r in res.results]
    y_prompt = np.stack(ys[0:4], axis=0)
    y_sample = np.stack([np.concatenate([ys[4 + 2 * b], ys[5 + 2 * b][::-1]], axis=0) for b in range(2)], axis=0)
    return (y_prompt, y_sample)
```

```python
import numpy as np
from contextlib import ExitStack
import concourse.bass as bass
import concourse.mybir as mybir
from concourse.bass_utils import run_bass_kernel_spmd

F32 = mybir.dt.float32
BF16 = mybir.dt.bfloat16
AF = mybir.ActivationFunctionType
ALU = mybir.AluOpType

D = 2048
T = 2048
NB = 2
TE = T + 128 * NB
NT = 16
NTE = NT + NB
TA = T + 128
NTA = NT + 1
DFF = 5632
NFT = DFF // 128
EPS = 1e-6
SCALE = 128 ** -0.5
NW = 4
NCORES = 8
DEBUG = False


class Tk:
    __slots__ = ("sem", "val", "eng")

    def __init__(self, sem, val, eng):
        self.sem = sem
        self.val = val
        self.eng = eng


class Buf:
    __slots__ = ("w", "r")

    def __init__(self):
        self.w = None
        self.r = {}


class DSem:
    __slots__ = ("sem", "val")

    def __init__(self, sem):
        self.sem = sem
        self.val = 0


class Ctx:
    ENG = ("pe", "act", "dve", "pool", "sp")

    def __init__(self, nc, es):
        self.nc = nc
        self.es = es
        self.sem = {e: es.enter_context(nc.semaphore("s_" + e)) for e in self.ENG}
        self.cnt = {e: 0 for e in self.ENG}
        self.seen = {e: {} for e in self.ENG}
        self.dsems = []
        self.prog = {e: [] for e in self.ENG}
        self.nsem = 0

    def dsem(self):
        self.nsem += 1
        d = DSem(self.es.enter_context(self.nc.semaphore("d%d" % self.nsem)))
        self.dsems.append(d)
        return d

    def _wait(self, e, tk):
        if tk is None:
            return
        k = id(tk.sem)
        if self.seen[e].get(k, 0) >= tk.val:
            return
        self.prog[e].append(lambda g, s=tk.sem, v=tk.val: g.wait_ge(s, v))
        self.seen[e][k] = tk.val

    def _deps(self, e, r, w):
        for b in r:
            self._wait(e, b.w)
        for b in w:
            if b.w is not None and (b.w.eng != e or e == "pool"):
                self._wait(e, b.w)
            for tk in b.r.values():
                if tk.eng != e or e == "pool":
                    self._wait(e, tk)

    def _mark(self, tk, r, w):
        for b in w:
            b.w = tk
            b.r = {}
        for b in r:
            b.r[id(tk.sem)] = tk

    def op(self, e, fn, r=(), w=()):
        self._deps(e, r, w)
        self.cnt[e] += 1
        self.prog[e].append(lambda g, fn=fn, s=self.sem[e]: fn(g).then_inc(s, 1))
        tk = Tk(self.sem[e], self.cnt[e], e)
        self._mark(tk, r, w)
        return tk

    def dma(self, q, ds, out, in_, r=(), w=()):
        self._deps(q, r, w)
        ds.val += 16
        self.prog[q].append(lambda g, o=out, i=in_, s=ds.sem: g.dma_start(out=o, in_=i).then_inc(s, 16))
        tk = Tk(ds.sem, ds.val, "dma")
        self._mark(tk, r, w)
        return tk

    def barrier(self, engines=None):
        for e in (engines or self.ENG):
            for d in self.dsems:
                if d.val:
                    self._wait(e, Tk(d.sem, d.val, "dma"))
            for x in self.ENG:
                if x != e and self.cnt[x]:
                    self._wait(e, Tk(self.sem[x], self.cnt[x], x))

    def emit(self):
        with self.nc.Block() as block:
            def mk(e):
                def f(g):
                    for c in self.prog[e]:
                        c(g)
                return f
            block.tensor(mk("pe"))
            block.scalar(mk("act"))
            block.vector(mk("dve"))
            block.gpsimd(mk("pool"))
            block.sync(mk("sp"))


class Ring:
    def __init__(self, items):
        self.items = items
        self.i = 0

    def next(self):
        it = self.items[self.i % len(self.items)]
        self.i += 1
        return it


def build_program():
    nc = bass.Bass("TRN2", target_bir_lowering=False)

    def din(name, shape, dt=F32):
        return nc.dram_tensor(name, list(shape), dt, kind="ExternalInput").ap()

    x_loc = din("x_loc", [TE, D])
    w_in_t = din("w_in_t", [136, 128, 2048])
    w_out_t = din("w_out_t", [16, 128, 2048])
    wg_t = din("wg_t", [NFT, 128, 2048])
    wu_t = din("wu_t", [NFT, 128, 2048])
    wd_t = din("wd_t", [NFT, 2, 128, 1024])
    vecs = din("vecs", [128, 7 * 16])
    gbc = din("gbc", [128, 2 * D])
    rope_cs = din("rope_cs", [32, 2 * TA])
    cmat = din("cmat", [128, 7 * 128])
    ropeP = din("ropeP", [32, 32])
    y_loc = nc.dram_tensor("y_loc", [T, D], F32, kind="ExternalOutput").ap()
    mg_d = (nc.dram_tensor("mg_d", [16, 128, T], BF16, kind="ExternalOutput") if DEBUG else nc.dram_tensor("mg_d", [16, 128, T], BF16)).ap()
    dbg_ds = []

    def dbg(c, name, ap, shape, dt, bufs):
        if not DEBUG:
            return
        o = nc.dram_tensor("dbg_" + name, list(shape), dt, kind="ExternalOutput").ap()
        if not dbg_ds:
            dbg_ds.append(c.dsem())
        c.dma("sp", dbg_ds[0], o, ap, r=bufs)

    with ExitStack() as es:
        c = Ctx(nc, es)

        def sb(name, shape, dt, stack=None):
            return (stack or es).enter_context(nc.sbuf_tensor(name, list(shape), dt))

        pst = es.enter_context(nc.psum_tensor("ps", [128, 4096], F32))
        psb = [Buf() for _ in range(8)]
        psi = [0]

        def psn():
            k = psi[0] % 8
            psi[0] += 1
            return pst[:, k * 512:(k + 1) * 512], psb[k]

        b_cst = Buf()
        ds_c = c.dsem()
        vec_sb = sb("vec_sb", [128, 7 * 16], F32)
        c.dma("sp", ds_c, vec_sb[:], vecs[:, :], w=[b_cst])
        cm = sb("cm", [128, 7 * 128], BF16)
        ds_c2 = c.dsem()
        c.dma("pool", ds_c2, cm[:], cmat[:, :], w=[b_cst])
        rp = sb("rp", [32, 32], BF16)
        c.dma("pool", ds_c2, rp[:], ropeP[:, :], w=[b_cst])
        ident = cm[:, 0:128]
        ones_bf = cm[:, 128:256]
        mask_f = cm[:, 256:384]
        mask_b = cm[:, 384:512]
        mask_pn = cm[:, 512:768]
        mask_prev = cm[:, 512:640]
        mask_next = cm[:, 640:768]
        mask_halo = cm[:, 768:896]
        gmp = vec_sb[:, 0:16]
        gfp = vec_sb[:, 16:32]
        rn = vec_sb[:, 64:80]
        cst2 = sb("cst2", [128, 7 * 16 + 4], F32)
        lb = cst2[:, 0:16]
        oml = cst2[:, 16:32]
        esk = cst2[:, 32:48]
        tmpc = cst2[:, 48:64]
        epsc = cst2[:, 112:113]
        c.op("dve", lambda g: g.tensor_tensor(out=tmpc, in0=vec_sb[:, 32:48], in1=vec_sb[:, 48:64], op=ALU.subtract), r=[b_cst], w=[b_cst])
        c.op("act", lambda g: g.activation(out=lb, in_=tmpc, func=AF.Sigmoid), r=[b_cst], w=[b_cst])
        c.op("act", lambda g: g.activation(out=oml, in_=tmpc, func=AF.Sigmoid, scale=-1.0), r=[b_cst], w=[b_cst])
        c.op("act", lambda g: g.activation(out=esk, in_=vec_sb[:, 80:96], func=AF.Exp), r=[b_cst], w=[b_cst])
        c.op("pool", lambda g: g.memset(epsc, EPS), w=[b_cst])
        c0c = cst2[:, 64:80]
        c1c = cst2[:, 80:96]
        rnh = cst2[:, 96:112]
        epsc = cst2[:, 112:113]
        onec = cst2[:, 113:114]
        eps4c = cst2[:, 114:115]
        c.op("pool", lambda g: g.memset(epsc, EPS), w=[b_cst])
        c.op("pool", lambda g: g.memset(onec, 1.0), w=[b_cst])
        c.op("pool", lambda g: g.memset(eps4c, 4.0 * EPS), w=[b_cst])
        c.op("dve", lambda g: g.tensor_scalar(out=c1c, in0=oml, scalar1=0.5, scalar2=0.0, op0=ALU.mult, op1=ALU.add), r=[b_cst], w=[b_cst])
        c.op("dve", lambda g: g.tensor_tensor(out=c0c, in0=lb, in1=c1c, op=ALU.add), r=[b_cst], w=[b_cst])
        c.op("dve", lambda g: g.tensor_scalar(out=rnh, in0=rn, scalar1=0.25, scalar2=0.0, op0=ALU.mult, op1=ALU.add), r=[b_cst], w=[b_cst])
        mhalf = sb("mhalf", [128, 512], F32)
        c.op("pool", lambda g: g.memset(mhalf[:], -0.5), w=[b_cst])
        zeros = sb("zeros", [128, 128], F32)
        c.op("pool", lambda g: g.memset(zeros[:], 0.0), w=[b_cst])

        with ExitStack() as es1:
            xnT = sb("xnT", [128, 16, TE], BF16, es1)
            b_xnT = Buf()

            with ExitStack() as esa:
                xb = [(sb("xb%d" % i, [128, D], F32, esa), Buf(), c.dsem()) for i in range(2)]
                xs = [(sb("xs%d" % i, [128, D], BF16, esa), Buf()) for i in range(2)]
                junk = sb("junkA", [128, D], BF16, esa)
                b_junk = Buf()
                ssA = sb("ssA", [128, 2 * NTE], F32, esa)
                b_ssA = Buf()
                for t in range(NTE):
                    xt, bxt, dsx = xb[t % 2]
                    xst, bxs = xs[t % 2]
                    c.dma("sp", dsx, xt[:], x_loc[t * 128:(t + 1) * 128, :], w=[bxt])
                    c.op("act", lambda g, xt=xt, t=t: g.activation(out=junk[:], in_=xt[:], func=AF.Square, accum_out=ssA[:, t:t + 1]),
                         r=[bxt], w=[b_junk, b_ssA])
                    c.op("dve", lambda g, t=t: g.tensor_scalar(out=ssA[:, NTE + t:NTE + t + 1], in0=ssA[:, t:t + 1], scalar1=1.0 / D, scalar2=EPS, op0=ALU.mult, op1=ALU.add),
                         r=[b_ssA], w=[b_ssA])
                    c.op("pool", lambda g, t=t: g.tensor_tensor(out=ssA[:, NTE + t:NTE + t + 1], in0=ssA[:, NTE + t:NTE + t + 1], in1=mhalf[:, 0:1], op=ALU.pow),
                         r=[b_ssA, b_cst], w=[b_ssA])
                    c.op("pool", lambda g, xt=xt, xst=xst, t=t: g.tensor_scalar(out=xst[:], in0=xt[:], scalar1=ssA[:, NTE + t:NTE + t + 1], scalar2=0.0, op0=ALU.mult, op1=ALU.add),
                         r=[bxt, b_ssA], w=[bxs])
                    for half in range(2):
                        pb, bpb = psn()
                        pbb = pb.bitcast(BF16)
                        for k in range(8):
                            kt = half * 8 + k
                            c.op("pe", lambda g, pbb=pbb, k=k, kt=kt, xst=xst: g.transpose(out=pbb[:, k * 128:(k + 1) * 128], in_=xst[:, kt * 128:(kt + 1) * 128], identity=ident),
                                 r=[bxs, b_cst], w=[bpb])
                        c.op("dve", lambda g, pbb=pbb, half=half, t=t: g.tensor_tensor(
                            out=xnT[:, half * 8:(half + 1) * 8, t * 128:(t + 1) * 128],
                            in0=pbb.rearrange("p (k t) -> p k t", k=8),
                            in1=gmp[:, half * 8:(half + 1) * 8].unsqueeze(2).to_broadcast([128, 8, 128]), op=ALU.mult),
                            r=[bpb, b_cst], w=[b_xnT])
                c.barrier()
            dbg(c, "xnT", xnT[:, :, 0:512], [128, 16, 512], BF16, [b_xnT])

            with ExitStack() as esp:
                def sb1(name, shape, dt):
                    return sb(name, shape, dt, esp)
                wring = Ring([(sb1("w%d" % i, [128, 16, 128], BF16), Buf(), c.dsem()) for i in range(NW)])
                cs_sb = sb1("cs_sb", [32, 2 * TA], F32)
                c.dma("sp", ds_c, cs_sb[:], rope_cs[:, :], w=[b_cst])
                cosT = cs_sb[:, 0:TA]
                sinT = cs_sb[:, TA:2 * TA]
                kT = sb1("kT", [128, TA], BF16); b_kTg = [Buf() for _ in range(5)]
                v_tok = sb1("v_tok", [128, NTA, 128], BF16); b_v = Buf()
                qT = sb1("qT", [128, T], BF16); b_qTg = [Buf() for _ in range(4)]
                AT = sb1("AT", [128, T], BF16); b_AT = Buf()
                ptr = Ring([(sb1("pt%d" % i, [128, 384], BF16), Buf()) for i in range(3)])
                rdr = Ring([(sb1("rd%d" % i, [128, 128], F32), Buf()) for i in range(2)])
                rt1 = Ring([(sb1("rt1_%d" % i, [32, 512], F32), Buf()) for i in range(1)])
                rt2 = Ring([(sb1("rt2_%d" % i, [32, 512], F32), Buf()) for i in range(1)])
                i_tok = sb1("i_tok", [128, NTE, 128], BF16); b_it = Buf()
                qs = sb1("qs", [128, T], BF16); b_qs = Buf()
                Fr = Ring([(sb1("F%d" % i, [128, 512], F32), Buf()) for i in range(2)])
                Er = Ring([(sb1("E%d" % i, [128, 512], F32), Buf()) for i in range(2)])
                QT = sb1("QT", [128, T], BF16); b_QT = Buf()
                KT = sb1("KT", [128, TE], BF16); b_KT = Buf()
                KK = sb1("KK", [128, NTE, 128], BF16); b_KK = Buf()
                dec = sb1("dec", [128, NTE], F32); b_dec = Buf()
                AM = sb1("AM", [128, NT, 128], BF16); b_AM = Buf()
                SB = sb1("SB", [128, NT, 128], BF16); b_SB = Buf()
                Ur = [(sb1("U%d" % i, [128, 128], F32), Buf()) for i in range(2)]
                O = sb1("O", [128, T], F32); b_O = Buf()
                SQr = Ring([(sb1("SQ%d" % i, [128, 512], BF16), Buf()) for i in range(4)])
                RSr = Ring([(sb1("RS%d" % i, [128, 512], F32), Buf()) for i in range(4)])
                G1r = Ring([(sb1("G1_%d" % i, [128, 512], F32), Buf()) for i in range(1)])
                G2r = Ring([(sb1("G2_%d" % i, [128, 512], F32), Buf()) for i in range(1)])
                G3r = Ring([(sb1("G3_%d" % i, [128, 512], F32), Buf()) for i in range(1)])
                TAr = Ring([(sb1("TA_%d" % i, [128, 512], F32), Buf()) for i in range(1)])
                TBr = Ring([(sb1("TB_%d" % i, [128, 512], F32), Buf()) for i in range(1)])
                MG = sb1("MG", [128, T], BF16); b_MG = Buf()
                ds_mg = c.dsem()
                b_mgd = Buf()

                wseq = []
                for ch in range(16):
                    wseq += [24 + ch, 40 + ch, 72 + ch, 56 + ch]
                    if ch % 4 == 0:
                        wseq += [16 + ch // 4, 20 + ch // 4]
                    wseq += [ch, 88 + ch, 120 + ch, 104 + ch]
                wstate = {"issued": 0, "used": 0, "slots": {}}

                def issue_w():
                    i = wstate["issued"]
                    if i >= len(wseq):
                        return
                    wt, bw, dsw = wring.next()
                    c.dma("pool", dsw, wt[:].rearrange("p k j -> p (k j)"), w_in_t[wseq[i]], w=[bw])
                    wstate["slots"][i] = (wt, bw)
                    wstate["issued"] += 1

                def load_w(n, ahead=NW - 1):
                    i = wstate["used"]
                    assert wseq[i] == n, (i, wseq[i], n)
                    while wstate["issued"] < min(len(wseq), i + ahead + 1):
                        issue_w()
                    wstate["used"] += 1
                    return wstate["slots"].pop(i)

                def proj_group(wt, bw, g0, n):
                    pb, bpb = psn()
                    for kt in range(16):
                        c.op("pe", lambda g, pb=pb, kt=kt, g0=g0, n=n, wt=wt: g.matmul(
                            pb[:, 0:n], lhsT=wt[:, kt, :], rhs=xnT[:, kt, g0:g0 + n], start=(kt == 0), stop=(kt == 15)),
                            r=[bw, b_xnT], w=[bpb])
                    return pb, bpb

                def proj_fm(wt, bw, ntok, evac):
                    for g0 in range(0, ntok, 512):
                        n = min(512, ntok - g0)
                        pb, bpb = proj_group(wt, bw, g0, n)
                        evac(pb, bpb, g0, n)

                def proj_tm(wt, bw, ntiles, evac):
                    for j0 in range(0, ntiles, 4):
                        nj = min(4, ntiles - j0)
                        pb, bpb = psn()
                        for jj in range(nj):
                            tt = j0 + jj
                            for kt in range(16):
                                c.op("pe", lambda g, pb=pb, jj=jj, tt=tt, kt=kt, wt=wt: g.matmul(
                                    pb[:, jj * 128:(jj + 1) * 128], lhsT=xnT[:, kt, tt * 128:(tt + 1) * 128], rhs=wt[:, kt, :],
                                    start=(kt == 0), stop=(kt == 15)), r=[bw, b_xnT], w=[bpb])
                        evac(pb, bpb, j0, nj)

                def rope(buf, bbufs, ntok):
                    for g0 in range(0, ntok, 512):
                        n = min(512, ntok - g0)
                        bbuf = bbufs[g0 // 512]
                        pb, bpb = psn()
                        c.op("pe", lambda g, pb=pb, g0=g0, n=n, buf=buf: g.matmul(pb[0:32, 0:n], lhsT=rp[:, :], rhs=buf[0:32, g0:g0 + n], start=True, stop=True),
                             r=[bbuf, b_cst], w=[bpb])
                        t1, bt1 = rt1.next()
                        t2, bt2 = rt2.next()
                        c.op("dve", lambda g, pb=pb, g0=g0, n=n, t1=t1: g.tensor_tensor(out=t1[:, 0:n], in0=pb[0:32, 0:n], in1=sinT[:, g0:g0 + n], op=ALU.mult),
                             r=[bpb, b_cst], w=[bt1])
                        c.op("pool", lambda g, g0=g0, n=n, t2=t2, buf=buf: g.tensor_tensor(out=t2[:, 0:n], in0=buf[0:32, g0:g0 + n], in1=cosT[:, g0:g0 + n], op=ALU.mult),
                             r=[bbuf, b_cst], w=[bt2])
                        c.op("dve", lambda g, g0=g0, n=n, t1=t1, t2=t2, buf=buf: g.tensor_tensor(out=buf[0:32, g0:g0 + n], in0=t1[:, 0:n], in1=t2[:, 0:n], op=ALU.add),
                             r=[bt1, bt2], w=[bbuf])

                b_SBt = [Buf() for _ in range(NT)]

                def proj_fm_gen(wt, bw, ntok, evac, pend=None):
                    for g0 in range(0, ntok, 512):
                        n = min(512, ntok - g0)
                        pb, bpb = proj_group(wt, bw, g0, n)
                        d = evac(pb, bpb, g0, n)
                        if d is not None:
                            pend.append(d)
                        yield

                def proj_tm_gen(wt, bw, ntiles, evac):
                    for j0 in range(0, ntiles, 4):
                        nj = min(4, ntiles - j0)
                        pb, bpb = psn()
                        for jj in range(nj):
                            tt = j0 + jj
                            for kt in range(16):
                                c.op("pe", lambda g, pb=pb, jj=jj, tt=tt, kt=kt, wt=wt: g.matmul(
                                    pb[:, jj * 128:(jj + 1) * 128], lhsT=xnT[:, kt, tt * 128:(tt + 1) * 128], rhs=wt[:, kt, :],
                                    start=(kt == 0), stop=(kt == 15)), r=[bw, b_xnT], w=[bpb])
                        evac(pb, bpb, j0, nj)
                        yield

                def attn_gen(ch):
                    def stage1(n):
                        slots = []
                        if n >= 1:
                            slots.append(n - 1)
                        slots.append(n + 1)
                        slots.append(n)
                        ns = len(slots)
                        pS, bS = psn()
                        for si, kb in enumerate(slots):
                            c.op("pe", lambda g, pS=pS, si=si, kb=kb, n=n: g.matmul(
                                pS[:, si * 128:(si + 1) * 128], lhsT=kT[:, kb * 128:(kb + 1) * 128], rhs=qT[:, n * 128:(n + 1) * 128],
                                start=True, stop=True), r=[b_kTg[kb // 4], b_qTg[n // 4]], w=[bS])
                        PT, bPT = ptr.next()
                        c.op("act", lambda g, PT=PT, pS=pS, ns=ns: g.activation(out=PT[:, 0:ns * 128], in_=pS[:, 0:ns * 128], func=AF.Exp, scale=SCALE),
                             r=[bS], w=[bPT])
                        if n == 0:
                            c.op("pool", lambda g, PT=PT: g.tensor_tensor(out=PT[:, 0:128], in0=PT[:, 0:128], in1=mask_next, op=ALU.mult), r=[bPT, b_cst], w=[bPT])
                        elif n < NT - 1:
                            c.op("pool", lambda g, PT=PT: g.tensor_tensor(out=PT[:, 0:256], in0=PT[:, 0:256], in1=mask_pn, op=ALU.mult), r=[bPT, b_cst], w=[bPT])
                        else:
                            c.op("pool", lambda g, PT=PT: g.tensor_tensor(out=PT[:, 0:128], in0=PT[:, 0:128], in1=mask_prev, op=ALU.mult), r=[bPT, b_cst], w=[bPT])
                            c.op("pool", lambda g, PT=PT: g.tensor_tensor(out=PT[:, 128:256], in0=PT[:, 128:256], in1=mask_halo, op=ALU.mult), r=[bPT, b_cst], w=[bPT])
                        return slots, PT, bPT

                    def stage2(n, slots, PT, bPT):
                        ns = len(slots)
                        pO, bO = psn()
                        for si, kb in enumerate(slots):
                            c.op("pe", lambda g, pO=pO, si=si, kb=kb, PT=PT, ns=ns: g.matmul(
                                pO[:, 0:128], lhsT=v_tok[:, kb, :], rhs=PT[:, si * 128:(si + 1) * 128], start=(si == 0), stop=(si == ns - 1)),
                                r=[b_v, bPT], w=[bO])
                        for si, kb in enumerate(slots):
                            c.op("pe", lambda g, pO=pO, si=si, PT=PT, ns=ns: g.matmul(
                                pO[:, 128:256], lhsT=ones_bf, rhs=PT[:, si * 128:(si + 1) * 128], start=(si == 0), stop=(si == ns - 1)),
                                r=[b_cst, bPT], w=[bO])
                        rd, brd = rdr.next()
                        c.op("dve", lambda g, rd=rd, pO=pO, ch=ch: g.tensor_scalar_add(out=rd[:], in0=pO[:, 128:256], scalar1=esk[:, ch:ch + 1]),
                             r=[bO, b_cst], w=[brd])
                        c.op("dve", lambda g, rd=rd: g.reciprocal(out=rd[:], in_=rd[:]), r=[brd], w=[brd])
                        c.op("dve", lambda g, rd=rd, pO=pO, n=n: g.tensor_tensor(out=AT[:, n * 128:(n + 1) * 128], in0=pO[:, 0:128], in1=rd[:], op=ALU.mult),
                             r=[bO, brd], w=[b_AT])

                    pend = {}
                    for n in range(NT + 2):
                        if n < NT:
                            pend[n] = stage1(n)
                        if n - 2 >= 0:
                            stage2(n - 2, *pend.pop(n - 2))
                        yield

                def hgrn_gen(ch):
                    wt, bw = load_w(24 + ch)

                    def qs_evac(pb, bpb, g0, n):
                        Tt, bT = G1r.next()
                        c.op("act", lambda g: g.activation(out=Tt[:, 0:n], in_=pb[:, 0:n], func=AF.Tanh, scale=0.5), r=[bpb], w=[bT])
                        c.op("dve", lambda g: g.scalar_tensor_tensor(out=qs[:, g0:g0 + n], in0=Tt[:, 0:n], scalar=1.0, in1=pb[:, 0:n], op0=ALU.add, op1=ALU.mult),
                             r=[bT, bpb], w=[b_qs])
                        return None
                    yield from proj_fm_gen(wt, bw, T, qs_evac)

                    for p in (1, 2):
                        ntok = T if p == 1 else TE
                        wt, bw = load_w((40 if p == 1 else 56) + ch)
                        rev = (p == 2)

                        def gate_evac(pb, bpb, g0, n, rev=rev):
                            Fg, bF = Fr.next()
                            Eg, bE = Er.next()
                            nj = n // 128
                            j0 = g0 // 128
                            c.op("act", lambda g: g.activation(out=Fg[:, 0:n], in_=pb[:, 0:n], func=AF.Tanh, scale=0.5), r=[bpb], w=[bF])
                            c.op("act", lambda g: g.activation(out=Fg[:, 0:n], in_=Fg[:, 0:n], func=AF.Identity, scale=c1c[:, ch:ch + 1], bias=c0c[:, ch:ch + 1]),
                                 r=[bF, b_cst], w=[bF])
                            for jj in range(nj):
                                if rev:
                                    c.op("dve", lambda g, jj=jj: g.tensor_tensor_scan(
                                        out=Eg[:, jj * 128:(jj + 1) * 128][:, ::-1], data0=Fg[:, jj * 128:(jj + 1) * 128][:, ::-1],
                                        data1=zeros[:, 0:128], initial=1.0, op0=ALU.mult, op1=ALU.add), r=[bF, b_cst], w=[bE])
                                else:
                                    c.op("dve", lambda g, jj=jj: g.tensor_tensor_scan(
                                        out=Eg[:, jj * 128:(jj + 1) * 128], data0=Fg[:, jj * 128:(jj + 1) * 128],
                                        data1=zeros[:, 0:128], initial=1.0, op0=ALU.mult, op1=ALU.add), r=[bF, b_cst], w=[bE])
                            last = 0 if rev else 127
                            c.op("act", lambda g: g.activation(out=dec[:, j0:j0 + nj], in_=Eg[:, 0:n].rearrange("p (j t) -> p j t", t=128)[:, :, last], func=AF.Copy),
                                 r=[bE], w=[b_dec])
                            if g0 < T:
                                c.op("pool", lambda g: g.tensor_tensor(out=QT[:, g0:g0 + n], in0=qs[:, g0:g0 + n], in1=Eg[:, 0:n], op=ALU.mult),
                                     r=[b_qs, bE], w=[b_QT])
                            c.op("act", lambda g: g.activation(out=Fg[:, 0:n], in_=Fg[:, 0:n], func=AF.Identity, scale=-1.0, bias=onec),
                                 r=[bF, b_cst], w=[bF])
                            c.op("dve", lambda g: g.reciprocal(out=Eg[:, 0:n], in_=Eg[:, 0:n]), r=[bE], w=[bE])
                            c.op("pool", lambda g: g.tensor_tensor(out=KT[:, g0:g0 + n], in0=Fg[:, 0:n], in1=Eg[:, 0:n], op=ALU.mult), r=[bF, bE], w=[b_KT])

                            def part_b():
                                pt, bpt = psn()
                                ptb = pt.bitcast(BF16)
                                for jj in range(nj):
                                    c.op("pe", lambda g, jj=jj: g.transpose(out=ptb[:, jj * 128:(jj + 1) * 128], in_=KT[:, g0 + jj * 128:g0 + (jj + 1) * 128], identity=ident),
                                         r=[b_KT, b_cst], w=[bpt])
                                c.op("act", lambda g: g.activation(out=KK[:, j0:j0 + nj, :], in_=ptb[:, 0:nj * 128].rearrange("p (j d) -> p j d", j=nj), func=AF.Copy),
                                     r=[bpt], w=[b_KK])
                            return part_b

                        pend = []
                        yield from proj_fm_gen(wt, bw, ntok, gate_evac, pend)
                        if p == 1:
                            wt, bw = load_w(72 + ch)
                            yield from proj_tm_gen(wt, bw, NTE, lambda pb, bpb, j0, nj: c.op(
                                "act", lambda g: g.activation(out=i_tok[:, j0:j0 + nj, :], in_=pb[:, 0:nj * 128].rearrange("p (j d) -> p j d", j=nj), func=AF.Copy),
                                r=[bpb], w=[b_it]))
                        else:
                            if ch % 4 == 0:
                                h = ch // 4
                                wt, bw = load_w(16 + h)
                                yield from proj_fm_gen(wt, bw, TA, kv_evac)
                                wt, bw = load_w(20 + h)
                                rope(kT, b_kTg, TA)
                                yield from proj_tm_gen(wt, bw, NTA, lambda pb, bpb, j0, nj: c.op(
                                    "dve", lambda g: g.tensor_copy(out=v_tok[:, j0:j0 + nj, :], in_=pb[:, 0:nj * 128].rearrange("p (j d) -> p j d", j=nj)),
                                    r=[bpb], w=[b_v]))
                            wt, bw = load_w(ch)
                            yield from proj_fm_gen(wt, bw, T, q_evac)
                            rope(qT, b_qTg, T)
                            yield "ATTN"
                        for d in pend:
                            d()
                            yield

                        order = list(range(NT)) if p == 1 else list(range(NTE - 1, -1, -1))
                        maskp = mask_f if p == 1 else mask_b
                        for j0 in range(0, NT, 4):
                            pb, bpb = psn()
                            for jj in range(4):
                                j = j0 + jj
                                c.op("pe", lambda g, pb=pb, jj=jj, j=j: g.matmul(
                                    pb[:, jj * 128:(jj + 1) * 128], lhsT=KT[:, j * 128:(j + 1) * 128], rhs=QT[:, j * 128:(j + 1) * 128], start=True, stop=True),
                                    r=[b_KT, b_QT], w=[bpb])
                            c.op("dve", lambda g, pb=pb, j0=j0, maskp=maskp: g.tensor_tensor(
                                out=AM[:, j0:j0 + 4, :], in0=pb.rearrange("p (j t) -> p j t", j=4),
                                in1=maskp.unsqueeze(1).to_broadcast([128, 4, 128]), op=ALU.mult), r=[bpb, b_cst], w=[b_AM])
                            yield
                        pbM = {}
                        for idx in range(0, len(order), 4):
                            pb, bpb = psn()
                            for jj, j in enumerate(order[idx:idx + 4]):
                                c.op("pe", lambda g, pb=pb, jj=jj, j=j: g.matmul(
                                    pb[:, jj * 128:(jj + 1) * 128], lhsT=KK[:, j, :], rhs=i_tok[:, j, :], start=True, stop=True),
                                    r=[b_KK, b_it], w=[bpb])
                                pbM[j] = (pb[:, jj * 128:(jj + 1) * 128], bpb)
                        prev = None
                        for k, j in enumerate(order):
                            Mj, bM = pbM[j]
                            Uc, bUc = Ur[k % 2]
                            if prev is None:
                                c.op("dve", lambda g, Uc=Uc, Mj=Mj: g.tensor_copy(out=Uc[:], in_=Mj), r=[bM], w=[bUc])
                                if j < NT:
                                    c.op("pool", lambda g, j=j: g.memset(SB[:, j, :], 0.0), w=[b_SBt[j]])
                            else:
                                pj, Up, bUp = prev
                                if j < NT:
                                    c.op("act", lambda g, j=j, Up=Up, pj=pj: g.activation(out=SB[:, j, :], in_=Up[:], func=AF.Copy, scale=dec[:, pj:pj + 1]),
                                         r=[bUp, b_dec], w=[b_SBt[j]])
                                c.op("dve", lambda g, Uc=Uc, Up=Up, pj=pj, Mj=Mj: g.scalar_tensor_tensor(
                                    out=Uc[:], in0=Up[:], scalar=dec[:, pj:pj + 1], in1=Mj, op0=ALU.mult, op1=ALU.add),
                                    r=[bUp, b_dec, bM], w=[bUc])
                            prev = (j, Uc, bUc)
                        yield
                        ogroups = list(range(0, NT, 4)) if p == 1 else list(range(NT - 4, -1, -4))
                        for j0 in ogroups:
                            pb, bpb = psn()
                            for jj in range(4):
                                j = j0 + jj
                                c.op("pe", lambda g, pb=pb, jj=jj, j=j: g.matmul(
                                    pb[:, jj * 128:(jj + 1) * 128], lhsT=i_tok[:, j, :], rhs=AM[:, j, :], start=True, stop=False),
                                    r=[b_it, b_AM], w=[bpb])
                                c.op("pe", lambda g, pb=pb, jj=jj, j=j: g.matmul(
                                    pb[:, jj * 128:(jj + 1) * 128], lhsT=SB[:, j, :], rhs=QT[:, j * 128:(j + 1) * 128], start=False, stop=True),
                                    r=[b_SBt[j], b_QT], w=[bpb])
                            if p == 1:
                                c.op("act", lambda g, pb=pb, j0=j0: g.activation(out=O[:, j0 * 128:(j0 + 4) * 128], in_=pb, func=AF.Copy), r=[bpb], w=[b_O])
                            else:
                                c.op("dve", lambda g, pb=pb, j0=j0: g.tensor_tensor(out=O[:, j0 * 128:(j0 + 4) * 128], in0=pb, in1=O[:, j0 * 128:(j0 + 4) * 128], op=ALU.add),
                                     r=[bpb, b_O], w=[b_O])
                            yield

                def drain(gen):
                    for _ in gen:
                        pass

                def kv_evac(pb, bpb, g0, n):
                    c.op("act", lambda g: g.activation(out=kT[:, g0:g0 + n], in_=pb[:, 0:n], func=AF.Copy), r=[bpb], w=[b_kTg[g0 // 512]])
                    return None

                def q_evac(pb, bpb, g0, n):
                    c.op("act", lambda g: g.activation(out=qT[:, g0:g0 + n], in_=pb[:, 0:n], func=AF.Copy), r=[bpb], w=[b_qTg[g0 // 512]])
                    return None

                for ch in range(16):
                    hg = hgrn_gen(ch)
                    at = None
                    for ev in hg:
                        if ev == "ATTN":
                            at = attn_gen(ch)
                        elif at is not None:
                            next(at, None)
                    if ch == 0 and DEBUG:
                        drain(at)
                        dbg(c, "AT", AT[:, :], [128, T], BF16, [b_AT])
                        dbg(c, "O", O[:, :], [128, T], F32, [b_O])
                        dbg(c, "dec2", dec[:, :], [128, NTE], F32, [b_dec])
                        dbg(c, "KK2", KK[:, :, :], [128, NTE, 128], BF16, [b_KK])
                        dbg(c, "KT2", KT[:, :], [128, TE], BF16, [b_KT])
                        dbg(c, "QT2", QT[:, :], [128, T], BF16, [b_QT])
                        dbg(c, "it", i_tok[:, :, :], [128, NTE, 128], BF16, [b_it])
                        dbg(c, "SB2", SB[:, :, :], [128, NT, 128], BF16, b_SBt)

                    wg_, bwg_ = load_w(88 + ch)
                    wr_, bwr_ = load_w(120 + ch, ahead=NW - 2)
                    wa_, bwa_ = load_w(104 + ch, ahead=NW - 3)
                    rs4 = []
                    for gi in range(4):
                        g0 = gi * 512
                        SQ, bSQ = SQr.next()
                        RS, bRS = RSr.next()
                        c.op("act", lambda g, SQ=SQ, g0=g0: g.activation(out=SQ[:], in_=O[:, g0:g0 + 512], func=AF.Square), r=[b_O], w=[bSQ])
                        pb, bpb = psn()
                        c.op("pe", lambda g, pb=pb, SQ=SQ: g.matmul(pb, lhsT=ones_bf, rhs=SQ[:], start=True, stop=True), r=[bSQ, b_cst], w=[bpb])
                        rs4.append((RS, bRS, pb, bpb))
                    for RS, bRS, pb, bpb in rs4:
                        c.op("act", lambda g, pb=pb, RS=RS: g.activation(out=RS[:], in_=pb, func=AF.Sqrt, scale=1.0 / 512, bias=epsc), r=[bpb, b_cst], w=[bRS])
                    for RS, bRS, pb, bpb in rs4:
                        c.op("dve", lambda g, RS=RS: g.reciprocal(out=RS[:], in_=RS[:]), r=[bRS], w=[bRS])
                    for gi in range(4):
                        g0 = gi * 512
                        if gi == 2:
                            drain(at)
                        RS, bRS = rs4[gi][0], rs4[gi][1]
                        G1, bG1 = G1r.next()
                        G2, bG2 = G2r.next()
                        G3, bG3 = G3r.next()
                        TAt, bTA = TAr.next()
                        TBt, bTB = TBr.next()
                        pbg, bpbg = proj_group(wg_, bwg_, g0, 512)
                        if gi < 2:
                            next(at, None)
                        pbr, bpbr = proj_group(wr_, bwr_, g0, 512)
                        if gi < 2:
                            next(at, None)
                        c.op("act", lambda g, pbg=pbg, G1=G1: g.activation(out=G1[:], in_=pbg, func=AF.Tanh, scale=0.5), r=[bpbg], w=[bG1])
                        c.op("dve", lambda g, pbg=pbg, G1=G1: g.scalar_tensor_tensor(out=G1[:], in0=G1[:], scalar=1.0, in1=pbg, op0=ALU.add, op1=ALU.mult), r=[bG1, bpbg], w=[bG1])
                        c.op("dve", lambda g, TAt=TAt, RS=RS, g0=g0: g.tensor_tensor(out=TAt[:], in0=O[:, g0:g0 + 512], in1=RS[:], op=ALU.mult), r=[b_O, bRS], w=[bTA])
                        c.op("dve", lambda g, TAt=TAt, G1=G1, ch=ch: g.scalar_tensor_tensor(out=TAt[:], in0=TAt[:], scalar=rnh[:, ch:ch + 1], in1=G1[:], op0=ALU.mult, op1=ALU.mult),
                             r=[bTA, bG1, b_cst], w=[bTA])
                        pba, bpba = proj_group(wa_, bwa_, g0, 512)
                        c.op("act", lambda g, pbr=pbr, G2=G2: g.activation(out=G2[:], in_=pbr, func=AF.Tanh, scale=0.5), r=[bpbr], w=[bG2])
                        c.op("act", lambda g, G2=G2: g.activation(out=G2[:], in_=G2[:], func=AF.Identity, bias=onec), r=[bG2, b_cst], w=[bG2])
                        c.op("pool", lambda g, TAt=TAt, G2=G2: g.tensor_tensor(out=TAt[:], in0=TAt[:], in1=G2[:], op=ALU.mult), r=[bTA, bG2], w=[bTA])
                        c.op("act", lambda g, pba=pba, G3=G3: g.activation(out=G3[:], in_=pba, func=AF.Tanh, scale=0.5), r=[bpba], w=[bG3])
                        c.op("act", lambda g, G3=G3: g.activation(out=G3[:], in_=G3[:], func=AF.Identity, bias=onec), r=[bG3, b_cst], w=[bG3])
                        c.op("pool", lambda g, TBt=TBt, G3=G3, g0=g0: g.tensor_tensor(out=TBt[:], in0=AT[:, g0:g0 + 512], in1=G3[:], op=ALU.mult), r=[b_AT, bG3], w=[bTB])
                        c.op("pool", lambda g, TAt=TAt, TBt=TBt, g0=g0: g.tensor_tensor(out=MG[:, g0:g0 + 512], in0=TAt[:], in1=TBt[:], op=ALU.add), r=[bTA, bTB], w=[b_MG])
                    c.dma("sp", ds_mg, mg_d[ch], MG[:], r=[b_MG], w=[b_mgd])

                c.barrier()

        with ExitStack() as es2:
            def sb2(name, shape, dt):
                return sb(name, shape, dt, es2)
            R32 = sb2("R32", [128, 8192], F32)
            R32b = R32[:].bitcast(BF16)
            MGs_t = sb2("MGs", [128, 16, 512], BF16)
            MGs = MGs_t[:, :, :]
            hnT = R32b[:, 8192:16384].rearrange("p (k t) -> p k t", k=16)
            FB = R32[:].rearrange("p (t d) -> p t d", t=4)
            b_MGs = Buf(); b_hnT = Buf(); b_FB = Buf()
            HB = sb2("HB", [128, 4, D], F32); b_HB = [Buf() for _ in range(4)]
            actT = sb2("actT", [128, NFT, 512], BF16); b_act = Buf()
            XBr = Ring([(sb2("XB%d" % i, [128, D], F32), Buf(), c.dsem()) for i in range(2)])
            gbc_sb = sb2("gbc_sb", [128, 2 * D], F32)
            c.dma("sp", ds_c, gbc_sb[:], gbc[:, :], w=[b_cst])
            g1bc = gbc_sb[:, 0:D]
            g2bc = gbc_sb[:, D:2 * D]
            HSr = Ring([(sb2("HS%d" % i, [128, D], BF16), Buf()) for i in range(2)])
            junk2 = sb2("junk2", [128, D], BF16); b_j2 = Buf()
            st2 = sb2("st2", [128, 32], F32); b_st2 = Buf()
            SLr = Ring([(sb2("SL%d" % i, [128, 512], F32), Buf()) for i in range(2)])
            w4 = Ring([(sb2("w4_%d" % i, [128, 16, 128], BF16), Buf(), c.dsem()) for i in range(4)])
            w2 = Ring([(sb2("w2_%d" % i, [128, 1024], BF16), Buf(), c.dsem()) for i in range(4)])
            ds_mgl = c.dsem()
            ds_y = [c.dsem() for _ in range(4)]

            seq2 = []
            for G in range(4):
                for pair in range(2):
                    for cc in range(16):
                        seq2.append(("o", cc, pair))
                for ft in range(NFT):
                    seq2.append(("g", ft, 0))
                    seq2.append(("u", ft, 0))
                for half in range(2):
                    for ft in range(NFT):
                        seq2.append(("d", ft, half))
            st = {"issued": 0, "used": 0, "slots": {}}
            DEPTH2 = 8

            def issue2(need=False):
                i = st["issued"]
                if i >= len(seq2):
                    return False
                kind, a, b = seq2[i]
                if kind in ("g", "u", "o"):
                    if w4.i - st.get("u4", 0) >= (4 if need else 3):
                        return False
                    wt, bw, dsw = w4.next()
                    src = (wg_t if kind == "g" else wu_t if kind == "u" else w_out_t)[a]
                    c.dma("pool", dsw, wt[:].rearrange("p k j -> p (k j)"), src, w=[bw])
                else:
                    if w2.i - st.get("u2", 0) >= (4 if need else 3):
                        return False
                    wt, bw, dsw = w2.next()
                    src = wd_t[a, b]
                    c.dma("pool", dsw, wt[:], src, w=[bw])
                st["slots"][i] = (wt, bw)
                st["issued"] += 1
                return True

            def load2(kind, a, b):
                i = st["used"]
                assert seq2[i] == (kind, a, b), (seq2[i], kind, a, b)
                while st["issued"] <= i:
                    assert issue2(True)
                st["used"] += 1
                if kind in ("g", "u", "o"):
                    st["u4"] = st.get("u4", 0) + 1
                else:
                    st["u2"] = st.get("u2", 0) + 1
                while st["issued"] < min(len(seq2), i + DEPTH2):
                    if not issue2():
                        break
                return st["slots"].pop(i)

            def rstd_from(ss_col, out_col, bst, eps=EPS):
                epa = eps4c if eps != EPS else epsc
                c.op("act", lambda g: g.activation(out=st2[:, out_col:out_col + 1], in_=st2[:, ss_col:ss_col + 1], func=AF.Sqrt, scale=1.0 / D, bias=epa),
                     r=[bst, b_cst], w=[bst])
                c.op("dve", lambda g: g.reciprocal(out=st2[:, out_col:out_col + 1], in_=st2[:, out_col:out_col + 1]), r=[bst], w=[bst])

            for G in range(4):
                t0 = G * 512
                if G == 0:
                    c.dma("sp", ds_mgl, MGs, mg_d.rearrange("c p t -> p c t")[:, :, t0:t0 + 512], r=[b_mgd], w=[b_MGs])
                def h_chain(tt):
                    XB, b_XB, ds_x = XBr.next()
                    HS, b_HS = HSr.next()
                    bst = Buf()
                    k0 = tt * 8
                    c.dma("sp", ds_x, XB[:], x_loc[t0 + tt * 128:t0 + (tt + 1) * 128, :], w=[b_XB])
                    c.op("act", lambda g, tt=tt, k0=k0: g.activation(out=junk2[:], in_=HB[:, tt, :], func=AF.Square, accum_out=st2[:, k0:k0 + 1]), r=[b_HB[tt]], w=[b_j2, bst])
                    rstd_from(k0, k0 + 1, bst, 4.0 * EPS)
                    c.op("dve", lambda g, tt=tt, k0=k0: g.scalar_tensor_tensor(out=HB[:, tt, :], in0=HB[:, tt, :], scalar=st2[:, k0 + 1:k0 + 2], in1=g1bc, op0=ALU.mult, op1=ALU.mult),
                         r=[b_HB[tt], bst, b_cst], w=[b_HB[tt]])
                    c.op("dve", lambda g, tt=tt, XB=XB: g.tensor_tensor(out=HB[:, tt, :], in0=HB[:, tt, :], in1=XB[:], op=ALU.add), r=[b_HB[tt], b_XB], w=[b_HB[tt]])
                    yield
                    c.op("act", lambda g, tt=tt, k0=k0: g.activation(out=junk2[:], in_=HB[:, tt, :], func=AF.Square, accum_out=st2[:, k0 + 2:k0 + 3]), r=[b_HB[tt]], w=[b_j2, bst])
                    rstd_from(k0 + 2, k0 + 3, bst)
                    c.op("act", lambda g, tt=tt, k0=k0, HS=HS: g.activation(out=HS[:], in_=HB[:, tt, :], func=AF.Copy, scale=st2[:, k0 + 3:k0 + 4]), r=[b_HB[tt], bst], w=[b_HS])
                    for half in range(2):
                        pb, bpb = psn()
                        pbb = pb.bitcast(BF16)
                        for k in range(8):
                            kt = half * 8 + k
                            c.op("pe", lambda g, pbb=pbb, k=k, kt=kt, HS=HS: g.transpose(out=pbb[:, k * 128:(k + 1) * 128], in_=HS[:, kt * 128:(kt + 1) * 128], identity=ident),
                                 r=[b_HS, b_cst], w=[bpb])
                        c.op("dve", lambda g, pbb=pbb, half=half, tt=tt: g.tensor_tensor(
                            out=hnT[:, half * 8:(half + 1) * 8, tt * 128:(tt + 1) * 128], in0=pbb.rearrange("p (k t) -> p k t", k=8),
                            in1=gfp[:, half * 8:(half + 1) * 8].unsqueeze(2).to_broadcast([128, 8, 128]), op=ALU.mult),
                            r=[bpb, b_cst], w=[b_hnT, b_FB])

                for pair in range(2):
                    tts = (2 * pair, 2 * pair + 1)
                    banks = {tt: [psn() for cg in range(4)] for tt in tts}
                    for cc in range(16):
                        wt, bw = load2("o", cc, pair)
                        wv = wt[:].rearrange("p k j -> p (k j)")
                        for tt in tts:
                            for cg in range(4):
                                pb, bpb = banks[tt][cg]
                                c.op("pe", lambda g, pb=pb, wv=wv, cc=cc, tt=tt, cg=cg: g.matmul(
                                    pb, lhsT=MGs[:, cc, tt * 128:(tt + 1) * 128], rhs=wv[:, cg * 512:(cg + 1) * 512], start=(cc == 0), stop=(cc == 15)),
                                    r=[bw, b_MGs], w=[bpb])
                    for tt in tts:
                        for cg in range(4):
                            pb, bpb = banks[tt][cg]
                            col = cg * 512
                            if cg % 2 == 0:
                                c.op("act", lambda g, pb=pb, tt=tt, col=col: g.activation(out=HB[:, tt, col:col + 512], in_=pb, func=AF.Copy), r=[bpb], w=[b_HB[tt]])
                            else:
                                c.op("dve", lambda g, pb=pb, tt=tt, col=col: g.tensor_copy(out=HB[:, tt, col:col + 512], in_=pb), r=[bpb], w=[b_HB[tt]])
                    if pair == 1 and G < 3:
                        c.dma("sp", ds_mgl, MGs, mg_d.rearrange("c p t -> p c t")[:, :, t0 + 512:t0 + 1024], r=[b_mgd], w=[b_MGs])
                    gens = [h_chain(tt) for tt in tts]
                    for gn in gens:
                        next(gn)
                    for gn in gens:
                        for _ in gn:
                            pass
                if G == 0:
                    dbg(c, "h", HB[:, :, :], [128, 4, D], F32, b_HB)
                    dbg(c, "hnT", hnT, [128, 16, 512], BF16, [b_hnT])
                for ft in range(NFT):
                    wgt, bwg = load2("g", ft, 0)
                    pg, bpg = psn()
                    for kt in range(16):
                        c.op("pe", lambda g, pg=pg, wgt=wgt, kt=kt: g.matmul(pg, lhsT=wgt[:, kt, :], rhs=hnT[:, kt, :], start=(kt == 0), stop=(kt == 15)),
                             r=[bwg, b_hnT], w=[bpg])
                    wut, bwu = load2("u", ft, 0)
                    pu, bpu = psn()
                    for kt in range(16):
                        c.op("pe", lambda g, pu=pu, wut=wut, kt=kt: g.matmul(pu, lhsT=wut[:, kt, :], rhs=hnT[:, kt, :], start=(kt == 0), stop=(kt == 15)),
                             r=[bwu, b_hnT], w=[bpu])
                    SL, bSL = SLr.next()
                    c.op("act", lambda g, SL=SL, pg=pg: g.activation(out=SL[:], in_=pg, func=AF.Tanh, scale=0.5), r=[bpg], w=[bSL])
                    c.op("dve", lambda g, SL=SL, pg=pg: g.scalar_tensor_tensor(out=SL[:], in0=SL[:], scalar=1.0, in1=pg, op0=ALU.add, op1=ALU.mult), r=[bSL, bpg], w=[bSL])
                    c.op("dve", lambda g, SL=SL, pu=pu, ft=ft: g.scalar_tensor_tensor(out=actT[:, ft, :], in0=SL[:], scalar=0.5, in1=pu, op0=ALU.mult, op1=ALU.mult), r=[bSL, bpu], w=[b_act])
                for half in range(2):
                    banks = [[psn() for cg in range(2)] for tt in range(4)]
                    for ft in range(NFT):
                        wt, bw = load2("d", ft, half)
                        for tt in range(4):
                            for cg in range(2):
                                pb, bpb = banks[tt][cg]
                                c.op("pe", lambda g, pb=pb, wt=wt, ft=ft, tt=tt, cg=cg: g.matmul(
                                    pb, lhsT=actT[:, ft, tt * 128:(tt + 1) * 128], rhs=wt[:, cg * 512:(cg + 1) * 512], start=(ft == 0), stop=(ft == NFT - 1)),
                                    r=[bw, b_act], w=[bpb])
                    for tt in range(4):
                        for cg in range(2):
                            pb, bpb = banks[tt][cg]
                            col = half * 1024 + cg * 512
                            c.op("act", lambda g, pb=pb, tt=tt, col=col: g.activation(out=FB[:, tt, col:col + 512], in_=pb, func=AF.Copy),
                                 r=[bpb], w=[b_FB, b_hnT])
                if G == 0:
                    dbg(c, "ffn", FB, [128, 4, D], F32, [b_FB])
                for tt in range(4):
                    bst = Buf()
                    k0 = tt * 8 + 4
                    c.op("act", lambda g, tt=tt, k0=k0: g.activation(out=junk2[:], in_=FB[:, tt, :], func=AF.Square, accum_out=st2[:, k0:k0 + 1]), r=[b_FB], w=[b_j2, bst])
                    rstd_from(k0, k0 + 1, bst)
                    c.op("dve", lambda g, tt=tt, k0=k0: g.scalar_tensor_tensor(out=FB[:, tt, :], in0=FB[:, tt, :], scalar=st2[:, k0 + 1:k0 + 2], in1=g2bc, op0=ALU.mult, op1=ALU.mult),
                         r=[b_FB, bst, b_cst], w=[b_FB, b_hnT])
                    c.op("dve", lambda g, tt=tt: g.tensor_tensor(out=FB[:, tt, :], in0=FB[:, tt, :], in1=HB[:, tt, :], op=ALU.add),
                         r=[b_FB, b_HB[tt]], w=[b_FB, b_hnT])
                    c.dma("sp", ds_y[tt], y_loc[t0 + tt * 128:t0 + (tt + 1) * 128, :], FB[:, tt, :], r=[b_FB])
            c.barrier(["sp"])
        c.emit()
    return nc


_CACHE = {}


def _host_consts():
    s = np.arange(128)[:, None]
    t = np.arange(128)[None, :]
    ident = (s == t)
    ones = np.ones((128, 128), bool)
    mask_f = (t >= s)
    mask_b = (t <= s)
    mask_prev = (t <= s)
    mask_next = (s <= t)
    return [m.astype(np.float32) for m in (ident, ones, mask_f, mask_b, mask_prev, mask_next)], mask_next.astype(np.float32)


def kernel(x_prompt, x_sample, w_in, sink, rec_norm, lb_logits, w_out, norm_mix_pre,
           norm_mix_post, norm_ffn_pre, norm_ffn_post, w_gate, w_up, w_down):
    f32 = np.float32
    x_prompt = np.asarray(x_prompt, f32)
    x_sample = np.asarray(x_sample, f32)
    w_in = np.asarray(w_in, f32)[0]
    w_out = np.asarray(w_out, f32)[0]
    w_gate = np.asarray(w_gate, f32)[0]
    w_up = np.asarray(w_up, f32)[0]
    w_down = np.asarray(w_down, f32)[0]

    w_in_t = np.ascontiguousarray(w_in.reshape(16, 128, 136, 128).transpose(2, 1, 0, 3)).reshape(136, 128, 2048)
    w_in_t_sw = w_in_t.copy()
    w_in_t_sw[40:56] = w_in_t[56:72]
    w_in_t_sw[56:72] = w_in_t[40:56]
    w_out_t = np.ascontiguousarray(w_out.reshape(16, 128, 2048))
    wg_t = np.ascontiguousarray(w_gate.reshape(16, 128, NFT, 128).transpose(2, 1, 0, 3)).reshape(NFT, 128, 2048)
    wu_t = np.ascontiguousarray(w_up.reshape(16, 128, NFT, 128).transpose(2, 1, 0, 3)).reshape(NFT, 128, 2048)
    wd_t = np.ascontiguousarray(w_down.reshape(NFT, 128, 2, 1024).transpose(0, 2, 1, 3))

    def pk(v):
        return np.asarray(v, f32).reshape(16, 128).T
    vecs = np.zeros((128, 7 * 16), f32)
    vecs[:, 0:16] = pk(norm_mix_pre[0])
    vecs[:, 16:32] = pk(norm_ffn_pre[0])
    vecs[:, 32:48] = pk(lb_logits[0])
    vecs[:, 48:64] = pk(lb_logits[1])
    vecs[:, 64:80] = np.asarray(rec_norm, f32)[0].T
    vecs[:, 80:96] = np.broadcast_to(np.asarray(sink, f32)[0][None, :], (128, 16))
    gbc = np.concatenate([np.broadcast_to(np.asarray(norm_mix_post, f32)[0][None, :], (128, D)),
                          np.broadcast_to(np.asarray(norm_ffn_post, f32)[0][None, :], (128, D))], axis=1).astype(f32)
    gbc = np.ascontiguousarray(gbc)
    mats, tri_next = _host_consts()
    ropeP = np.zeros((32, 32), f32)
    for m in range(16):
        ropeP[m + 16, m] = -1.0
        ropeP[m, m + 16] = 1.0
    inv_freq = (500000.0 ** (-np.arange(16, dtype=np.float32) / 16)).astype(np.float32)

    in_maps = []
    zeros_ext = np.zeros((TE - T, D), f32)
    for core in range(NCORES):
        if core < 4:
            x_loc = np.concatenate([x_prompt[core], zeros_ext], axis=0)
            pos = np.arange(TA, dtype=np.float32)
            halo = np.zeros((128, 128), f32)
            wi = w_in_t
        else:
            b = (core - 4) // 2
            second = (core - 4) % 2 == 1
            if not second:
                x_loc = x_sample[b, 0:TE]
                pos = np.arange(TA, dtype=np.float32)
                wi = w_in_t
            else:
                x_loc = x_sample[b, ::-1][0:TE]
                pos = (4095 - np.arange(TA)).astype(np.float32)
                wi = w_in_t_sw
            halo = tri_next
        ang = pos[None, :] * np.tile(inv_freq, 2)[:, None]
        rope_cs = np.concatenate([np.cos(ang), np.sin(ang)], axis=1).astype(f32)
        cmat = np.concatenate(mats + [halo], axis=1).astype(f32)
        in_maps.append({
            "x_loc": np.ascontiguousarray(x_loc, dtype=f32), "w_in_t": wi, "w_out_t": w_out_t, "wg_t": wg_t, "wu_t": wu_t,
            "wd_t": wd_t, "vecs": vecs, "gbc": gbc, "rope_cs": np.ascontiguousarray(rope_cs), "cmat": np.ascontiguousarray(cmat),
            "ropeP": ropeP,
        })

    if DEBUG:
        return in_maps
    if "nc" not in _CACHE:
        _CACHE["nc"] = build_program()
    res = run_bass_kernel_spmd(_CACHE["nc"], in_maps, core_ids=list(range(NCORES)))
    ys = [np.asarray(r["y_loc"], f32) for r in res.results]
    y_prompt = np.stack(ys[0:4], axis=0)
    y_sample = np.stack([np.concatenate([ys[4 + 2 * b], ys[5 + 2 * b][::-1]], axis=0) for b in range(2)], axis=0)
    return (y_prompt, y_sample)
```
